# Optimizing a Trainium2 kernel written in Bass

```python
import jax, jax.numpy as jnp
from jax import lax
import numpy as np

D_MODEL = 4096
BATCH = 2
SEQ = 8192
DEPTH = 1

MEM_LEN = 256
MIX_WIDTH = D_MODEL
POOL_WIDTH = MIX_WIDTH // 2
ATTN_WIDTH = MIX_WIDTH - POOL_WIDTH
POOL_WINDOWS = (2, 4, 8, 16)
N_POOL_GROUPS = len(POOL_WINDOWS)
POOL_GROUP_WIDTH = POOL_WIDTH // N_POOL_GROUPS
HEAD_DIM = 128
MOBA_HEADS = ATTN_WIDTH // HEAD_DIM
MOBA_BLOCK = 256
MOBA_TOPK = 3
MOBA_Q_CHUNK = 32
MIX_IN_WIDTH = POOL_WIDTH + 3 * ATTN_WIDTH
ROPE_THETA = 10000.0
XATTN_HEADS = 4
XATTN_HEAD_DIM = D_MODEL // XATTN_HEADS
FFN_HIDDEN = -((-8 * D_MODEL) // (3 * 256)) * 256
LN_EPS = 1e-5
DEEPNORM_ALPHA = (2 * DEPTH) ** 0.25
DEEPNORM_BETA = (8 * DEPTH) ** -0.25

kernel_name = 'hybrid_pool_moba_deepnorm_block'


def layer_norm(x, g, b):
    xf = x.astype(jnp.float32)
    mu = jnp.mean(xf, axis=-1, keepdims=True)
    var = jnp.mean(jnp.square(xf - mu), axis=-1, keepdims=True)
    return ((xf - mu) * lax.rsqrt(var + LN_EPS) * g + b).astype(x.dtype)


def rope(t, pos):
    half = t.shape[-1] // 2
    inv = ROPE_THETA ** (-jnp.arange(half, dtype=jnp.float32) / half)
    ang = pos.astype(jnp.float32)[:, None] * inv[None, :]
    cos, sin = jnp.cos(ang), jnp.sin(ang)
    t1 = t[..., :half].astype(jnp.float32)
    t2 = t[..., half:].astype(jnp.float32)
    return jnp.concatenate([t1 * cos - t2 * sin, t2 * cos + t1 * sin], axis=-1).astype(t.dtype)


def pool_mixer(u, w_pool, scale):
    B, S, _ = u.shape
    u = u.reshape(B, S, N_POOL_GROUPS, POOL_GROUP_WIDTH)
    uf = u.astype(jnp.float32)
    cs = jnp.pad(jnp.cumsum(uf, axis=1), ((0, 0), (1, 0), (0, 0), (0, 0)))
    win = jnp.array(POOL_WINDOWS, dtype=jnp.int32)
    t = jnp.arange(S, dtype=jnp.int32)
    lo = jnp.maximum(t[:, None] + 1 - win[None, :], 0)
    cnt = jnp.minimum(t[:, None] + 1, win[None, :]).astype(jnp.float32)
    grp = jnp.arange(N_POOL_GROUPS)[None, :]
    mean = (cs[:, 1:] - cs[:, lo, grp]) / cnt[None, :, :, None]
    mixed = (mean - uf).astype(u.dtype)
    y = jnp.einsum('bsgc,gcd->bsgd', mixed, w_pool) * scale.reshape(N_POOL_GROUPS, POOL_GROUP_WIDTH)
    return y.reshape(B, S, POOL_WIDTH)


def _gather_blocks(blocks, idx):
    return jax.vmap(jax.vmap(lambda bl, i: bl[i]))(blocks, idx)


def moba_attention(q, k, v):
    B, H, S, hd = q.shape
    nb = -(-S // MOBA_BLOCK)
    sp = nb * MOBA_BLOCK
    pad = ((0, 0), (0, 0), (0, sp - S), (0, 0))
    q, k, v = jnp.pad(q, pad), jnp.pad(k, pad), jnp.pad(v, pad)
    kb = k.reshape(B, H, nb, MOBA_BLOCK, hd)
    vb = v.reshape(B, H, nb, MOBA_BLOCK, hd)
    kmean = jnp.mean(kb.astype(jnp.float32), axis=3)
    gate = jnp.einsum('bhtd,bhnd->bhtn', q.astype(jnp.float32), kmean)
    qblk_all = jnp.arange(sp) // MOBA_BLOCK
    gate = jnp.where(jnp.arange(nb)[None, :] < qblk_all[:, None], gate, -jnp.inf)
    k_sel = min(MOBA_TOPK, nb)
    _, sel_idx = lax.top_k(gate, k_sel)
    scale = HEAD_DIM ** -0.5
    key_off = jnp.arange(MOBA_BLOCK)
    C = MOBA_Q_CHUNK

    def step(s0):
        qc = lax.dynamic_slice_in_dim(q, s0, C, axis=2)
        ic = lax.dynamic_slice_in_dim(sel_idx, s0, C, axis=2)
        blk = s0 // MOBA_BLOCK
        kown = lax.dynamic_index_in_dim(kb, blk, axis=2, keepdims=False)
        vown = lax.dynamic_index_in_dim(vb, blk, axis=2, keepdims=False)
        ksel = _gather_blocks(kb, ic)
        vsel = _gather_blocks(vb, ic)
        qpos = s0 + jnp.arange(C)
        sel_ok = jnp.arange(k_sel)[None, :] < (qpos // MOBA_BLOCK)[:, None]
        own_ok = (blk * MOBA_BLOCK + key_off)[None, :] <= qpos[:, None]
        s_sel = jnp.einsum('bhcd,bhckpd->bhckp', qc, ksel).astype(jnp.float32) * scale
        s_sel = jnp.where(sel_ok[None, None, :, :, None], s_sel, -jnp.inf)
        s_sel = s_sel.reshape(B, H, C, k_sel * MOBA_BLOCK)
        s_own = jnp.einsum('bhcd,bhpd->bhcp', qc, kown).astype(jnp.float32) * scale
        s_own = jnp.where(own_ok[None, None], s_own, -jnp.inf)
        p = jax.nn.softmax(jnp.concatenate([s_sel, s_own], axis=-1), axis=-1)
        p_sel = p[..., :k_sel * MOBA_BLOCK].reshape(B, H, C, k_sel, MOBA_BLOCK).astype(v.dtype)
        p_own = p[..., k_sel * MOBA_BLOCK:].astype(v.dtype)
        return (jnp.einsum('bhckp,bhckpd->bhcd', p_sel, vsel)
                + jnp.einsum('bhcp,bhpd->bhcd', p_own, vown))

    out = lax.map(step, jnp.arange(sp // C, dtype=jnp.int32) * C)
    out = out.transpose(1, 2, 0, 3, 4).reshape(B, H, sp, hd)
    return out[:, :, :S]


def hybrid_mixer(x, w_in, w_pool, pool_scale, w_out, pos):
    B, S, _ = x.shape
    h = x @ w_in
    hp = h[..., :POOL_WIDTH]
    hq = h[..., POOL_WIDTH:POOL_WIDTH + ATTN_WIDTH]
    hk = h[..., POOL_WIDTH + ATTN_WIDTH:POOL_WIDTH + 2 * ATTN_WIDTH]
    hv = h[..., POOL_WIDTH + 2 * ATTN_WIDTH:]
    pool_out = pool_mixer(hp, w_pool, pool_scale)

    def to_heads(t):
        return t.reshape(B, S, MOBA_HEADS, HEAD_DIM).transpose(0, 2, 1, 3)

    q = rope(to_heads(hq), pos)
    k = rope(to_heads(hk), pos)
    a = moba_attention(q, k, to_heads(hv))
    a = a.transpose(0, 2, 1, 3).reshape(B, S, ATTN_WIDTH)
    return jnp.concatenate([pool_out, a], axis=-1) @ w_out


def memory_cross_attention(x, mem, wq, wkv, wo):
    B, S, D = x.shape
    M = mem.shape[1]
    q = (x @ wq).reshape(B, S, XATTN_HEADS, XATTN_HEAD_DIM)
    kv = mem @ wkv
    k = kv[..., :D].reshape(B, M, XATTN_HEADS, XATTN_HEAD_DIM)
    v = kv[..., D:].reshape(B, M, XATTN_HEADS, XATTN_HEAD_DIM)
    s = jnp.einsum('bshd,bmhd->bhsm', q, k).astype(jnp.float32) * XATTN_HEAD_DIM ** -0.5
    p = jax.nn.softmax(s, axis=-1).astype(x.dtype)
    o = jnp.einsum('bhsm,bmhd->bshd', p, v).reshape(B, S, D)
    return o @ wo


def swiglu_ffn(x, w_gate, w_up, w_down):
    return (jax.nn.silu(x @ w_gate) * (x @ w_up)) @ w_down


def _normal(key, shape, std):
    return std * jax.random.normal(key, shape, dtype=jnp.float32)


def setup_inputs(seed: int = 0) -> dict:
    key = jax.random.key(seed)
    ks = jax.random.split(key, 18)
    L, D = DEPTH, D_MODEL
    return {
        'x': _normal(ks[0], (BATCH, SEQ, D), 1.0),
        'mem': _normal(ks[1], (BATCH, MEM_LEN, D), 1.0),
        'w_mix_in': _normal(ks[2], (L, D, MIX_IN_WIDTH), D ** -0.5),
        'w_pool': _normal(ks[3], (L, N_POOL_GROUPS, POOL_GROUP_WIDTH, POOL_GROUP_WIDTH), POOL_GROUP_WIDTH ** -0.5),
        'pool_scale': 1.0 + _normal(ks[4], (L, POOL_WIDTH), 0.1),
        'w_mix_out': _normal(ks[5], (L, MIX_WIDTH, D), DEEPNORM_BETA * MIX_WIDTH ** -0.5),
        'ln1_g': 1.0 + _normal(ks[6], (L, D), 0.02),
        'ln1_b': _normal(ks[7], (L, D), 0.02),
        'w_xq': _normal(ks[8], (L, D, D), D ** -0.5),
        'w_xkv': _normal(ks[9], (L, D, 2 * D), D ** -0.5),
        'w_xo': _normal(ks[10], (L, D, D), DEEPNORM_BETA * D ** -0.5),
        'ln2_g': 1.0 + _normal(ks[11], (L, D), 0.02),
        'ln2_b': _normal(ks[12], (L, D), 0.02),
        'w_gate': _normal(ks[13], (L, D, FFN_HIDDEN), D ** -0.5),
        'w_up': _normal(ks[14], (L, D, FFN_HIDDEN), D ** -0.5),
        'w_down': _normal(ks[15], (L, FFN_HIDDEN, D), DEEPNORM_BETA * FFN_HIDDEN ** -0.5),
        'ln3_g': 1.0 + _normal(ks[16], (L, D), 0.02),
        'ln3_b': _normal(ks[17], (L, D), 0.02),
    }


def reference(x, mem, w_mix_in, w_pool, pool_scale, w_mix_out, ln1_g, ln1_b,
              w_xq, w_xkv, w_xo, ln2_g, ln2_b, w_gate, w_up, w_down, ln3_g, ln3_b):
    pos = jnp.arange(x.shape[1], dtype=jnp.int32)
    h = x
    for l in range(DEPTH):
        mix = hybrid_mixer(h, w_mix_in[l], w_pool[l], pool_scale[l], w_mix_out[l], pos)
        h = layer_norm(DEEPNORM_ALPHA * h + mix, ln1_g[l], ln1_b[l])
        xa = memory_cross_attention(h, mem, w_xq[l], w_xkv[l], w_xo[l])
        h = layer_norm(DEEPNORM_ALPHA * h + xa, ln2_g[l], ln2_b[l])
        ff = swiglu_ffn(h, w_gate[l], w_up[l], w_down[l])
        h = layer_norm(DEEPNORM_ALPHA * h + ff, ln3_g[l], ln3_b[l])
    return h
```

```python
import os
import numpy as np
import concourse.bass as bass
import concourse.mybir as mybir
from concourse.bass_utils import run_bass_kernel_spmd

F32, BF16 = mybir.dt.float32, mybir.dt.bfloat16
AF = mybir.ActivationFunctionType
ALU = mybir.AluOpType
AX = mybir.AxisListType

D = 4096
KC = 32
T = 512
HEADS = 16
FFN = 11008
FC = 86
MEM = 256
BIG = 30000.0
ALPHA = 2.0 ** 0.25
EPS = 1e-5
NBLK = 32
WINS = (2, 4, 8, 16)


class Buf:
    __slots__ = ("name", "space", "lo", "hi", "ap", "overl", "last_w", "rd", "rd_dma")

    def __init__(self, name, space, lo, hi, ap):
        self.name, self.space, self.lo, self.hi, self.ap = name, space, lo, hi, ap
        self.overl = [self]
        self.last_w = None
        self.rd = {}
        self.rd_dma = []


class Op:
    __slots__ = ("eng", "fn", "deps", "dma", "cnt", "need_inc", "idx")


def _semname(n):
    if n.startswith("c_"):
        return "const"
    if n == "cast_in_kv":
        return "castA"
    if n.startswith("cast_"):
        return "castD"
    if n in ("cos", "sin"):
        return "cs"
    if n.startswith("kv_"):
        return "kv" + n[-1]
    if n.startswith("w_wslot"):
        return "w" + n[7]
    if n.startswith("w_fslot"):
        return "f" + str(int(n[7]) // 2)
    if n.startswith("yst_"):
        return "yst" + n.rstrip("_")[-1]
    if n in ("ln_g", "ln_b"):
        return "lngb"
    if n in ("kmTs", "vms"):
        return "xkv"
    if n.startswith("st_"):
        return "st" + n[-1]
    return n


class Prog:
    ENGS = ("pe", "act", "dve", "pool", "sp")

    def __init__(self):
        self.ops = {e: [] for e in self.ENGS}
        self.bufs = {}
        self.dma_cnt = {}
        self.nops = 0

    def buf(self, name, space, lo, nbytes, ap):
        b = Buf(name, space, lo, lo + nbytes, ap)
        lst = self.bufs.setdefault(space, [])
        for o in lst:
            if o.lo < b.hi and b.lo < o.hi:
                o.overl.append(b)
                b.overl.append(o)
        lst.append(b)
        return b

    frozen = False

    def op(self, eng, fn, reads=(), writes=(), dma=None):
        if self.frozen:
            return None
        if dma is not None:
            dma = _semname(dma)
        o = Op()
        o.eng, o.fn, o.dma, o.need_inc, o.cnt = eng, fn, dma, False, 0
        o.idx = self.nops
        self.nops += 1
        deps = {}

        def add(d):
            if d is None:
                return
            if d.dma is None and d.eng == "pe" and eng == "pe" and dma is None:
                return
            if d.dma is not None:
                deps[id(d)] = (d, self.dma_cnt[d.dma])
            else:
                deps[id(d)] = (d, None)

        for b in reads:
            for ob in b.overl:
                add(ob.last_w)
                if b.space == "ps":
                    for r in ob.rd.values():
                        if r.eng != eng:
                            add(r)
        for b in writes:
            for ob in b.overl:
                add(ob.last_w)
                for r in ob.rd.values():
                    add(r)
                for r in ob.rd_dma:
                    add(r)
        o.deps = list(deps.values())
        for d, _ in o.deps:
            d.need_inc = True
        for b in writes:
            b.last_w = o
            b.rd = {}
            b.rd_dma = []
        for b in reads:
            if dma is not None:
                b.rd_dma.append(o)
            else:
                b.rd[eng] = o
        if dma is not None:
            self.dma_cnt[dma] = self.dma_cnt.get(dma, 0) + 16
            o.cnt = self.dma_cnt[dma]
        self.ops[eng].append(o)
        return o

    def emit(self, nc, block, stack):
        sems = {}

        def sem(name):
            if name not in sems:
                sems[name] = stack.enter_context(nc.semaphore("s_" + name))
            return sems[name]

        cnt = {e: 0 for e in self.ENGS}
        allops = sorted((o for e in self.ENGS for o in self.ops[e]), key=lambda o: o.idx)
        for o in allops:
            if o.dma is None and o.need_inc:
                cnt[o.eng] += 1
                o.cnt = cnt[o.eng]
        for e in self.ENGS:
            sem("e_" + e)
        for name in self.dma_cnt:
            sem("d_" + name)

        final_op = self.final_op

        def run(eng_name):
            def body(e):
                waited = {}
                for o in self.ops[eng_name]:
                    need = {}
                    for d, v in o.deps:
                        if d.dma is not None:
                            k, val = "d_" + d.dma, v
                        else:
                            k, val = "e_" + d.eng, d.cnt
                        if val > need.get(k, 0):
                            need[k] = val
                    if o.dma is not None and o.cnt > 16:
                        k = "d_" + o.dma
                        need[k] = max(need.get(k, 0), o.cnt - 16)
                    wl = []
                    for k, val in need.items():
                        if waited.get(k, 0) < val:
                            e.wait_ge(sems[k], val)
                            waited[k] = val
                            wl.append((k, val))
                    if os.environ.get("KDUMP"):
                        print("OP", eng_name, o.idx, "waits", wl, "inc", (o.dma, o.cnt) if o.dma else (o.cnt if o.need_inc else None),
                              getattr(o, "tag", ""))
                    if o.fn is None:
                        if o is final_op:
                            for nm, tot in self.dma_cnt.items():
                                e.wait_ge(sems["d_" + nm], tot)
                        continue
                    ins = o.fn(e)
                    if o.dma is not None:
                        ins.then_inc(sems["d_" + o.dma], 16)
                    elif o.need_inc:
                        ins.then_inc(sems["e_" + eng_name], 1)
            return body

        block.tensor(run("pe"))
        block.scalar(run("act"))
        block.vector(run("dve"))
        block.gpsimd(run("pool"))
        block.sync(run("sp"))


def build(nctx=12, nown=4, debug=False):
    nc = bass.Bass("TRN2", target_bir_lowering=False)
    NT = nctx + nown
    NSLOT = NT * T
    CTXB = nctx * 2

    def din(name, shape, dt=F32):
        return nc.dram_tensor(name, shape, dt, kind="ExternalInput").ap()

    def dscr(name, shape, dt):
        return nc.dram_tensor(name, shape, dt, kind="Internal").ap()

    xall = din("xall", [NSLOT, D])
    mem = din("mem", [MEM, D])
    w_in = din("w_in", [D, 2 * D])
    w_pool = din("w_pool", [4 * 512, 512])
    w_out = din("w_out", [D, D])
    w_xq = din("w_xq", [D, D])
    w_xkv = din("w_xkv", [D, 2 * D])
    w_xo = din("w_xo", [D, D])
    w_gate = din("w_gate", [D, FFN])
    w_up = din("w_up", [D, FFN])
    w_down = din("w_down", [FFN, D])
    lnp = din("lnp", [6, D])
    pscale_d = din("pscale", [128, 16])
    cos_d = din("cosT", [128, NSLOT])
    sin_d = din("sinT", [128, NSLOT])
    invc_d = din("invc", [128, 4, nown * T])
    vbg_d = din("vbg", [128, 8, NBLK])
    ndg_d = din("ndg", [128, 8, NBLK])
    cm_d = din("cm", [128, 4, T])
    eb_d = din("eblk", [32, NBLK, 128])
    psw_d = din("pswap", [128, 128])
    idn_d = din("ident", [128, 128])
    out_d = nc.dram_tensor("out", [nown * T, D], F32, kind="ExternalOutput").ap()

    wb = {
        "in": dscr("wb_in", [D, 2 * D], BF16), "pool": dscr("wb_pool", [2048, 512], BF16),
        "out": dscr("wb_out", [D, D], BF16), "xq": dscr("wb_xq", [D, D], BF16),
        "xkv": dscr("wb_xkv", [D, 2 * D], BF16), "xo": dscr("wb_xo", [D, D], BF16),
        "gate": dscr("wb_gate", [D, FFN], BF16), "up": dscr("wb_up", [D, FFN], BF16),
        "down": dscr("wb_down", [FFN, D], BF16),
    }
    wsrc = {"in": w_in, "pool": w_pool, "out": w_out, "xq": w_xq, "xkv": w_xkv, "xo": w_xo,
            "gate": w_gate, "up": w_up, "down": w_down}
    kt_scr = dscr("kt_scr", [HEADS, 128, NSLOT], BF16)
    v_scr = dscr("v_scr", [NSLOT, 2048], BF16)
    kmT_scr = dscr("kmT_scr", [D, MEM], BF16)
    vm_scr = dscr("vm_scr", [MEM, D], BF16)
    y_scr = dscr("y_scr", [T, D], F32)
    h_scr = dscr("h_scr", [T, D], F32)

    P = Prog()
    import os
    STOP = int(os.environ.get("KSTOP", "99"))

    def mark(n):
        if STOP == n:
            P.frozen = True
    import contextlib
    stack = contextlib.ExitStack()
    ARENA = 204800
    E0 = 190464
    arena = stack.enter_context(nc.sbuf_tensor("arena", [128, ARENA // 4], F32))
    psum = stack.enter_context(nc.psum_tensor("psum", [128, 8, 512], F32))

    def sb(name, off, shape, dt=F32, parts=128):
        esz = 4 if dt == F32 else 2
        n = int(np.prod(shape))
        nbytes = n * esz
        assert off % 4 == 0 and off + nbytes <= ARENA, (name, off, nbytes)
        ap = arena[0:parts, off // 4:(off + nbytes + 3) // 4]
        if dt != F32:
            ap = ap.bitcast(dt)
        if len(shape) == 2:
            ap = ap.rearrange("p (a b) -> p a b", a=shape[0])
        elif len(shape) == 3:
            ap = ap.rearrange("p (a b c) -> p a b c", a=shape[0], b=shape[1])
        return P.buf(name, "sb", off, nbytes, ap)

    PB = [P.buf("ps%d" % i, "ps", i * 2048, 2048, psum[:, i, :]) for i in range(8)]
    dres = {}

    def dr(name):
        if name not in dres:
            dres[name] = P.buf(name, "dram:" + name, 0, 1, None)
        return dres[name]

    W0, W1, A0, B0, C0 = 0, 22528, 45056, 77824, 165888
    WS = 22528
    c = C0
    identf = sb("identf", c, [128]); c += 512
    identb = sb("identb", c, [128], BF16); c += 256
    pswap = sb("pswap", c, [128], BF16); c += 256
    ones = sb("ones", c, [128], BF16); c += 256
    eblk = sb("eblk", c, [NBLK, 128], BF16, parts=32); c += 8192
    cm = sb("cm", c, [4, T], BF16); c += 4096
    kmsum = sb("kmsum", c, [HEADS, NBLK]); c += 2048
    kmT = sb("kmT", c, [HEADS, NBLK], BF16); c += 1024
    vbg = sb("vbg", c, [8, NBLK]); c += 1024
    vbgm = sb("vbgm", c, [8, NBLK]); c += 1024
    ndg = sb("ndg", c, [8, NBLK]); c += 1024
    halo = sb("halo", c, [16, 16]); c += 1024
    pscale = sb("pscale", c, [16]); c += 64
    epsc = sb("epsc", c, [1]); c += 4
    g2 = sb("g2", c, [NBLK]); c += 128
    top8 = sb("top8", c, [8]); c += 32
    selb = sb("selb", c, [NBLK]); c += 128
    mbias = [sb("mbias%d" % i, c + 128 * i, [NBLK], BF16) for i in range(2)]; c += 256
    stats = sb("stats", c, [8, 6]); c += 192
    mv = sb("mv", c, [2]); c += 8
    rstd = sb("rstd", c, [1]); c += 4
    ksum2 = sb("ksum2", c, [2]); c += 8
    assert c <= ARENA, c

    wslot_l = [sb("wslot0", W0, [43, 256], BF16), sb("wslot1", W1, [43, 256], BF16)]
    wslot_s = [sb("wslot0s", W0, [32, 256], BF16), sb("wslot1s", W1, [32, 256], BF16)]
    fslot = [sb("fslot%d" % i, W0 + i * 8192, [32, 128], BF16) for i in range(4)]
    actA = sb("actA", A0, [KC, T], BF16)

    def v(b):
        return b.ap

    cast_q = []

    def cast_weight(name, c0=None, c1=None, tag=None):
        src = wsrc[name]
        rows = src.shape[0]
        if c0 is None:
            c0, c1 = 0, src.shape[1]
        tag = tag or name
        step = max(128, (8 << 20) // ((c1 - c0) * 4) // 128 * 128)
        r = 0
        while r < rows:
            e = min(rows, r + step)
            cast_q.append((name, r, e, c0, c1, tag))
            r = e

    def need_weight(tag):
        idx = [i for i, c in enumerate(cast_q) if c[5] == tag]
        if idx:
            issue_casts(idx[-1] + 1)

    def issue_casts(n, deps=()):
        for _ in range(min(n, len(cast_q))):
            name, r, e, c0, c1, tag = cast_q.pop(0)
            src, dst = wsrc[name], wb[name]
            P.op("pool", (lambda g, r=r, e=e, c0=c0, c1=c1, src=src, dst=dst: g.dma_start(out=dst[r:e, c0:c1], in_=src[r:e, c0:c1])),
                 reads=list(deps), writes=[dr("wb_" + tag)], dma="cast_" + tag)


    def load_const(dst, src_ap, shape, parts=128, conv=True, off=0):
        nel = int(np.prod(shape))
        st = sb("cst_%s" % dst.name, B0 + off, shape, F32, parts=parts)
        P.op("sp", lambda s: s.dma_start(out=v(st), in_=src_ap), writes=[st], dma="c_" + dst.name)
        conv_list.append((dst, st))
        return nel * 4

    o = 0
    conv_list = []
    P.op("sp", lambda s: s.dma_start(out=v(identf), in_=idn_d[:, :]), writes=[identf], dma="c_ident")
    o += load_const(pswap, psw_d[:, :], [128], off=o)
    o += load_const(eblk, eb_d[:, :, :], [NBLK, 128], parts=32, off=o)
    o += load_const(cm, cm_d[:, :, :], [4, T], off=o)
    P.op("sp", lambda s: s.dma_start(out=v(vbg), in_=vbg_d[:, :, :]), writes=[vbg], dma="c_vbg")
    P.op("sp", lambda s: s.dma_start(out=v(ndg), in_=ndg_d[:, :, :]), writes=[ndg], dma="c_ndg")
    P.op("sp", lambda s: s.dma_start(out=v(pscale), in_=pscale_d[:, :]), writes=[pscale], dma="c_pscale")
    P.op("dve", lambda e: e.tensor_copy(out=v(identb), in_=v(identf)), reads=[identf], writes=[identb])
    for dst_, st_ in conv_list:
        P.op("dve", lambda e, dst_=dst_, st_=st_: e.tensor_copy(out=v(dst_), in_=v(st_)), reads=[st_], writes=[dst_])
    P.op("dve", lambda e: e.tensor_scalar(out=v(vbgm), in0=v(vbg), scalar1=-BIG, scalar2=None, op0=ALU.add),
         reads=[vbg], writes=[vbgm])
    P.op("dve", lambda e: e.memset(v(ones), 1.0), writes=[ones])
    P.op("dve", lambda e: e.memset(v(epsc), EPS), writes=[epsc])
    P.op("dve", lambda e: e.memset(v(halo), 0.0), writes=[halo])
    P.op("dve", lambda e: e.memset(v(kmsum), 0.0), writes=[kmsum])
    P.op("dve", lambda e: e.memset(v(kmT), 0.0), writes=[kmT])

    cast_weight("in", 4096, 8192, "in_kv")
    issue_casts(len(cast_q))
    cast_weight("in", 0, 4096, "in_pq")
    cast_weight("pool")
    if nctx < 4:
        issue_casts(len(cast_q))
    for name in ("xkv", "out", "xq", "xo", "gate", "up", "down"):
        cast_weight(name)

    mark(1)
    psrr = [0]

    def next_ps(banks=(2, 3, 4, 5, 6, 7)):
        b = banks[psrr[0] % len(banks)]
        psrr[0] += 1
        return PB[b]

    evrr = [0]

    def ev_eng():
        evrr[0] += 1
        return "act" if evrr[0] % 2 else "dve"

    def copy_op(eng, dst_ap, src_ap, reads, writes):
        if eng == "act":
            P.op("act", lambda e: e.copy(out=dst_ap, in_=src_ap), reads=reads, writes=writes)
        else:
            P.op(eng, lambda e: e.tensor_copy(out=dst_ap, in_=src_ap), reads=reads, writes=writes)

    def transpose_rows(src_buf, src_ap, dstT, col0, ncols=128):
        for k4 in range(KC // 4):
            pb = PB[k4 % 2]

            def f(t, k4=k4, pb=pb):
                ins = None
                for j in range(4):
                    kc = k4 * 4 + j
                    ins = t.transpose(out=pb.ap[:, j * 128:j * 128 + ncols],
                                      in_=src_ap[0:ncols, kc * 128:(kc + 1) * 128],
                                      identity=v(identf)[0:ncols, 0:ncols])
                return ins
            P.op("pe", f, reads=[src_buf, identf], writes=[pb])
            copy_op(ev_eng(), v(dstT)[:, k4 * 4:(k4 + 1) * 4, col0:col0 + ncols],
                    pb.ap.rearrange("p (a b) -> p a b", a=4)[:, :, 0:ncols], [pb], [dstT])

    wrr = [0]

    def load_panel(name, k0, kn, c0, cn):
        slot = (wslot_s if kn <= 32 else wslot_l)[wrr[0] % 2]
        wrr[0] += 1
        src = wb[name][k0 * 128:(k0 + kn) * 128, c0:c0 + cn].rearrange("(k p) n -> p k n", p=128)
        rname = name if name != "in" else ("in_kv" if c0 >= 4096 else "in_pq")
        need_weight(rname)
        P.op("sp", lambda s: s.dma_start(out=slot.ap[:, 0:kn, 0:cn], in_=src),
             reads=[dr("wb_" + rname)], writes=[slot], dma="w_" + slot.name)
        return slot

    def proj_fm(name, actT, kn, c0, nchunks, consume, ntok=T, pair=None):
        n = 0
        pend = None
        panels = []
        for p0 in range(0, nchunks, 2):
            cn = min(2, nchunks - p0)
            panels.append((p0, cn))
        loaded = {}

        def ensure(i):
            if i < len(panels) and i not in loaded:
                p0, cn = panels[i]
                loaded[i] = load_panel(name, 0, kn, c0 + p0 * 128, cn * 128)
        ensure(0)
        for i, (p0, cn) in enumerate(panels):
            ensure(i + 1)
            slot = loaded.pop(i)
            for j in range(cn):
                pb = next_ps()

                def f(t, slot=slot, j=j, pb=pb):
                    ins = None
                    for kc in range(kn):
                        ins = t.matmul(pb.ap[:, 0:ntok], lhsT=slot.ap[:, kc, j * 128:(j + 1) * 128],
                                       rhs=actT.ap[:, kc, 0:ntok], start=(kc == 0), stop=(kc == kn - 1))
                    return ins
                P.op("pe", f, reads=[slot, actT], writes=[pb])
                consume(p0 + j, pb)

    def proj_tm(name, actT, kn, ystage, nsub=4, dst=None, dst_res=None, ncols=D, dst_bf16=None):
        halves = [(0, kn)] if kn <= 43 else [(0, 43), (43, kn - 43)]
        npan = ncols // 256
        seq = [(p, h) for p in range(npan) for h in range(len(halves))]
        loaded = {}

        def ensure(i):
            if i < len(seq) and i not in loaded:
                p, h = seq[i]
                loaded[i] = load_panel(name, halves[h][0], halves[h][1], p * 256, 256)
        ensure(0)
        si = 0
        two = len(halves) > 1
        for p in range(npan):
            pbs = [next_ps() for _ in range(nsub if two else (nsub + 1) // 2)]
            for h in range(len(halves)):
                ensure(si + 1)
                slot = loaded.pop(si)
                si += 1
                k0, kk = halves[h]

                def f(t, slot=slot, k0=k0, kk=kk, h=h, pbs=pbs):
                    ins = None
                    for sub in range(nsub):
                        if two:
                            out = pbs[sub].ap[:, 0:256]
                        else:
                            out = pbs[sub // 2].ap[:, (sub % 2) * 256:(sub % 2) * 256 + 256]
                        for kc in range(kk):
                            ins = t.matmul(out, lhsT=actT.ap[:, k0 + kc, sub * 128:(sub + 1) * 128],
                                           rhs=slot.ap[:, kc, 0:256],
                                           start=(h == 0 and kc == 0),
                                           stop=(h == len(halves) - 1 and kc == kk - 1))
                    return ins
                P.op("pe", f, reads=[slot, actT], writes=pbs)
            ys = ystage[p % 2]
            if two:
                for sub in range(nsub):
                    copy_op(ev_eng(), ys.ap[:, sub, :], pbs[sub].ap[:, 0:256], [pbs[sub]], [ys])
            else:
                for q in range((nsub + 1) // 2):
                    ns = min(2, nsub - 2 * q)
                    copy_op(ev_eng(), ys.ap[:, 2 * q:2 * q + ns, :],
                            pbs[q].ap.rearrange("p (a b) -> p a b", a=2)[:, 0:ns, :], [pbs[q]], [ys])
            dd = dst[0:nsub * 128, p * 256:(p + 1) * 256].rearrange("(s p) n -> p s n", p=128)
            P.op("sp", lambda s, ys=ys, dd=dd: s.dma_start(out=dd, in_=ys.ap[:, 0:nsub, :]),
                 reads=[ys], writes=[dst_res], dma="yst_" + ys.name)

    def ln_stage(y_res, resid_ap, resid_res, gi, out_ap, out_res, outT, lnb, nsub=4):
        ybufs, rbufs, gbuf, bbuf = lnb
        P.op("sp", lambda s: s.dma_start(out=v(gbuf), in_=lnp[gi, :].partition_broadcast(128)),
             writes=[gbuf], dma="ln_g")
        P.op("sp", lambda s: s.dma_start(out=v(bbuf), in_=lnp[gi + 1, :].partition_broadcast(128)),
             writes=[bbuf], dma="ln_b")

        def load(sub):
            ybuf, rbuf = ybufs[sub % 2], rbufs[sub % 2]
            rs = slice(sub * 128, (sub + 1) * 128)
            P.op("sp", lambda s: s.dma_start(out=v(ybuf), in_=y_scr[rs, :]),
                 reads=[y_res], writes=[ybuf], dma="ln_y%d" % (sub % 2))
            P.op("sp", lambda s: s.dma_start(out=v(rbuf), in_=resid_ap[rs, :]),
                 reads=[resid_res], writes=[rbuf], dma="ln_r%d" % (sub % 2))

        def stage1(sub):
            ybuf, rbuf = ybufs[sub % 2], rbufs[sub % 2]
            P.op("dve", lambda e: e.scalar_tensor_tensor(out=v(ybuf), in0=v(rbuf), scalar=ALPHA, in1=v(ybuf),
                                                         op0=ALU.mult, op1=ALU.add),
                 reads=[rbuf, ybuf], writes=[ybuf])

            def st(e):
                ins = None
                for i in range(8):
                    ins = e.bn_stats(out=v(stats)[:, i, :], in_=v(ybuf)[:, i * 512:(i + 1) * 512])
                return ins
            P.op("dve", st, reads=[ybuf], writes=[stats])
            P.op("dve", lambda e: e.bn_aggr(out=v(mv), in_=v(stats).rearrange("p a b -> p (a b)")),
                 reads=[stats], writes=[mv])
            P.op("act", lambda e: e.activation(out=v(rstd), in_=v(mv)[:, 1:2], func=AF.Sqrt, bias=v(epsc), scale=1.0),
                 reads=[mv, epsc], writes=[rstd])
            P.op("dve", lambda e: e.reciprocal(out=v(rstd), in_=v(rstd)), reads=[rstd], writes=[rstd])
            P.op("dve", lambda e: e.tensor_scalar(out=v(ybuf), in0=v(ybuf), scalar1=v(mv)[:, 0:1], scalar2=v(rstd),
                                                  op0=ALU.subtract, op1=ALU.mult),
                 reads=[ybuf, mv, rstd], writes=[ybuf])

        def stage2(sub):
            ybuf = ybufs[sub % 2]
            rs = slice(sub * 128, (sub + 1) * 128)
            P.op("dve", lambda e: e.tensor_tensor(out=v(ybuf), in0=v(ybuf), in1=v(gbuf), op=ALU.mult),
                 reads=[ybuf, gbuf], writes=[ybuf])
            P.op("dve", lambda e: e.tensor_tensor(out=v(ybuf), in0=v(ybuf), in1=v(bbuf), op=ALU.add),
                 reads=[ybuf, bbuf], writes=[ybuf])
            P.op("sp", lambda s: s.dma_start(out=out_ap[rs, :], in_=v(ybuf)),
                 reads=[ybuf], writes=[out_res], dma="ln_o")
            if outT is not None:
                transpose_rows(ybuf, v(ybuf), outT, sub * 128)

        load(0)
        if nsub > 1:
            load(1)
        stage1(0)
        for sub in range(nsub):
            if sub + 1 < nsub:
                stage1(sub + 1)
            stage2(sub)
            if sub + 2 < nsub:
                load(sub + 2)

    def ln_bufs():
        return ([sb("ln_y0", B0, [D]), sb("ln_y1", B0 + 32768, [D])],
                [sb("ln_r0", B0 + 16384, [D]), sb("ln_r1", B0 + 49152, [D])],
                sb("ln_g", B0 + 65536, [D]), sb("ln_b", W0, [D]))

    def mem_stage():
        memst = sb("memst", B0, [D])
        memT = sb("memT", A0, [KC, T], BF16)
        for sub in range(2):
            P.op("sp", lambda s, sub=sub: s.dma_start(out=v(memst), in_=mem[sub * 128:(sub + 1) * 128, :]),
                 writes=[memst], dma="xst0")
            transpose_rows(memst, v(memst), memT, sub * 128)
        kmstage = [sb("kmstage%d" % i, B0 + 16384 + i * 512, [MEM], BF16) for i in range(2)]

        def km_consume(n, pb):
            ks = kmstage[n % 2]
            copy_op(ev_eng(), v(ks), pb.ap[:, 0:MEM], [pb], [ks])
            P.op("sp", lambda s, ks=ks, n=n: s.dma_start(out=kmT_scr[n * 128:(n + 1) * 128, :], in_=v(ks)),
                 reads=[ks], writes=[dr("kmT_scr")], dma="st_" + ks.name)
        proj_fm("xkv", memT, KC, 0, KC, km_consume, ntok=MEM)
        ystage_b = [sb("ystB%d" % i, B0 + 20480 + i * 4096, [4, 256]) for i in range(2)]
        vmst = [sb("vmst%d" % i, B0 + 32768 + i * 1024, [2, 256], BF16) for i in range(2)]
        npan = D // 256
        loaded = {}

        def ens(i):
            if i < npan and i not in loaded:
                loaded[i] = load_panel("xkv", 0, KC, D + i * 256, 256)
        ens(0)
        for p in range(npan):
            ens(p + 1)
            slot = loaded.pop(p)
            pb = next_ps()

            def f(t, slot=slot, pb=pb):
                ins = None
                for sub in range(2):
                    for kc in range(KC):
                        ins = t.matmul(pb.ap[:, sub * 256:(sub + 1) * 256], lhsT=memT.ap[:, kc, sub * 128:(sub + 1) * 128],
                                       rhs=slot.ap[:, kc, 0:256], start=(kc == 0), stop=(kc == KC - 1))
                return ins
            P.op("pe", f, reads=[slot, memT], writes=[pb])
            vs = vmst[p % 2]
            copy_op(ev_eng(), v(vs), pb.ap.rearrange("p (a b) -> p a b", a=2), [pb], [vs])
            P.op("sp", lambda s, vs=vs, p=p: s.dma_start(
                out=vm_scr[:, p * 256:(p + 1) * 256].rearrange("(s p) n -> p s n", p=128), in_=v(vs)),
                reads=[vs], writes=[dr("vm_scr")], dma="st_" + vs.name)


    mark(2)
    xst = [sb("xst0", B0, [D]), sb("xst1", B0 + 16384, [D])]
    vst = sb("vst", B0, [4, 2048], BF16)
    mixedT = sb("mixedT", B0 + 16384, [16, T], BF16)
    qT = sb("qT", B0 + 32768, [HEADS, T], BF16)
    kst = sb("kst", B0 + 49152, [HEADS, T], BF16)
    cosb = sb("cosb", B0 + 65536, [T])
    sinb = sb("sinb", B0 + 67584, [T])
    invc = sb("invc", B0 + 69632, [4, T])
    qf = sb("qf", B0 + 77824, [T])
    qb = sb("qb", B0 + 79872, [T], BF16)
    t1 = sb("t1", B0 + 80896, [T])
    ropeset = [(qf, qb, t1), (sb("qf2", E0, [T]), sb("qb2", E0 + 2048, [T], BF16), sb("t12", E0 + 3072, [T]))]
    rrr = [0]
    hp = [sb("hp%d" % i, B0 + 82944 + i * 2112, [528]) for i in range(2)]

    def rope_consume(pb, dst_ap, dst_buf, ksum_h=None, blk0=None):
        qf, qb, t1 = ropeset[rrr[0] % 2]
        sw = PB[rrr[0] % 2]
        rrr[0] += 1
        P.op("act", lambda e: e.copy(out=v(qf), in_=pb.ap), reads=[pb], writes=[qf])
        P.op("dve", lambda e: e.tensor_copy(out=v(qb), in_=v(qf)), reads=[qf], writes=[qb])
        P.op("pe", lambda t: t.matmul(sw.ap, lhsT=v(pswap), rhs=v(qb), start=True, stop=True),
             reads=[pswap, qb], writes=[sw])
        P.op("dve", lambda e: e.tensor_tensor(out=v(t1), in0=sw.ap, in1=v(sinb), op=ALU.mult),
             reads=[sw, sinb], writes=[t1])
        P.op("dve", lambda e: e.tensor_tensor(out=v(qf), in0=v(qf), in1=v(cosb), op=ALU.mult),
             reads=[qf, cosb], writes=[qf])
        P.op("dve", lambda e: e.tensor_tensor(out=v(qf), in0=v(qf), in1=v(t1), op=ALU.add),
             reads=[qf, t1], writes=[qf])
        P.op("act", lambda e: e.copy(out=dst_ap, in_=v(qf)), reads=[qf], writes=[dst_buf])
        if ksum_h is not None:
            P.op("dve", lambda e: e.tensor_reduce(out=v(kmsum)[:, ksum_h, blk0:blk0 + 2],
                                                  in_=v(qf).rearrange("p (a b) -> p a b", a=2),
                                                  axis=AX.X, op=ALU.add),
                 reads=[qf], writes=[kmsum])

    def pool_chunk(ch, pb, save_only):
        h = hp[ch % 2]
        g = ch // 4
        w = WINS[g]
        P.op("act", lambda e: e.copy(out=v(h)[:, 16:528], in_=pb.ap), reads=[pb], writes=[h])
        P.op("dve", lambda e: e.tensor_copy(out=v(h)[:, 0:16], in_=v(halo)[:, ch, :]), reads=[halo], writes=[h])
        P.op("dve", lambda e: e.tensor_copy(out=v(halo)[:, ch, :], in_=v(h)[:, 512:528]), reads=[h], writes=[halo])
        if save_only:
            return
        sA, sB = pooltmp[ch % 2]
        P.op("dve", lambda e: e.tensor_tensor(out=v(sA)[:, 1:528], in0=v(h)[:, 1:528], in1=v(h)[:, 0:527], op=ALU.add),
             reads=[h], writes=[sA])
        cur, oth, width, lo = sA, sB, 2, 1
        while width < w:
            lo2 = lo + width
            P.op("dve", lambda e, cur=cur, oth=oth, width=width, lo2=lo2: e.tensor_tensor(
                out=v(oth)[:, lo2:528], in0=v(cur)[:, lo2:528], in1=v(cur)[:, lo2 - width:528 - width], op=ALU.add),
                reads=[cur], writes=[oth])
            cur, oth = oth, cur
            width *= 2
            lo = lo2
        assert lo <= 16
        P.op("dve", lambda e, cur=cur: e.tensor_tensor(out=v(cur)[:, 16:528], in0=v(cur)[:, 16:528],
                                                       in1=v(invc)[:, g, :], op=ALU.mult),
             reads=[cur, invc], writes=[cur])
        P.op("dve", lambda e, cur=cur: e.tensor_tensor(out=v(mixedT)[:, ch, :], in0=v(cur)[:, 16:528],
                                                      in1=v(h)[:, 16:528], op=ALU.subtract),
             reads=[cur, h], writes=[mixedT])

    pooltmp = [(sb("ptA%d" % i, W0 + 16384 + i * WS, [528]), sb("ptB%d" % i, W0 + 16384 + 2112 + i * WS, [528]))
               for i in range(2)]

    def do_tile(ti):
        own = ti >= nctx
        oi = ti - nctx
        s0 = ti * T
        xT = actA
        for sub in range(4):
            xs = xst[sub % 2]
            P.op("sp", lambda s, xs=xs, sub=sub: s.dma_start(out=v(xs), in_=xall[s0 + sub * 128:s0 + (sub + 1) * 128, :]),
                 writes=[xs], dma=xs.name)
            transpose_rows(xs, v(xs), xT, sub * 128)
        P.op("sp", lambda s: s.dma_start(out=v(cosb), in_=cos_d[:, s0:s0 + T]), writes=[cosb], dma="cos")
        P.op("sp", lambda s: s.dma_start(out=v(sinb), in_=sin_d[:, s0:s0 + T]), writes=[sinb], dma="sin")
        mark(31)
        if (own or ti == nctx - 1) and os.environ.get("KNOPOOL", "0") == "0":
            if own:
                P.op("sp", lambda s: s.dma_start(out=v(invc), in_=invc_d[:, :, oi * T:(oi + 1) * T]),
                     writes=[invc], dma="invc")
            proj_fm("in", xT, KC, 0, 16, lambda n, pb: pool_chunk(n, pb, not own))
        mark(32)
        if own:
            proj_fm("in", xT, KC, 2048, HEADS, lambda n, pb: rope_consume(pb, v(qT)[:, n, :], qT))
        proj_fm("in", xT, KC, int(os.environ.get("KC0", "4096")), HEADS,
                lambda n, pb: rope_consume(pb, v(kst)[:, n, :], kst, ksum_h=n, blk0=2 * ti))
        mark(33)
        P.op("sp", lambda s: s.dma_start(out=kt_scr[:, :, s0:s0 + T].rearrange("h d t -> d h t"), in_=v(kst)),
             reads=[kst], writes=[dr("kt_%d" % ti)], dma="kst")
        mark(34)
        loaded = {}

        def ensv(i):
            if i < 8 and i not in loaded:
                loaded[i] = load_panel("in", 0, KC, 6144 + i * 256, 256)
        ensv(0)
        for p in range(8):
            ensv(p + 1)
            slot = loaded.pop(p)
            pbs = [next_ps(), next_ps()]

            def f(t, slot=slot, pbs=pbs):
                ins = None
                for sub in range(4):
                    out = pbs[sub // 2].ap[:, (sub % 2) * 256:(sub % 2) * 256 + 256]
                    for kc in range(KC):
                        ins = t.matmul(out, lhsT=xT.ap[:, kc, sub * 128:(sub + 1) * 128], rhs=slot.ap[:, kc, 0:256],
                                       start=(kc == 0), stop=(kc == KC - 1))
                return ins
            P.op("pe", f, reads=[slot, xT], writes=pbs)
            for q in range(2):
                copy_op(ev_eng(), v(vst)[:, 2 * q:2 * q + 2, p * 256:(p + 1) * 256],
                        pbs[q].ap.rearrange("p (a b) -> p a b", a=2), [pbs[q]], [vst])
        P.op("sp", lambda s: s.dma_start(out=v_scr[s0:s0 + T, :].rearrange("(s p) n -> p s n", p=128), in_=v(vst)),
             reads=[vst], writes=[dr("v_%d" % ti)], dma="vst")
        P.op("dve", lambda e: e.tensor_scalar(out=v(kmT)[:, :, 2 * ti:2 * ti + 2], in0=v(kmsum)[:, :, 2 * ti:2 * ti + 2],
                                              scalar1=1.0 / 256.0, scalar2=None, op0=ALU.mult),
             reads=[kmsum], writes=[kmT])
        if not own:
            mark(3)
            return
        mark(4)
        issue_casts(6, [dr("kt_%d" % ti)])

        mixT = actA
        need_weight("pool")
        for g in range(4):
            slot = wslot_s[wrr[0] % 2]
            wrr[0] += 1
            P.op("sp", lambda s, slot=slot, g=g: s.dma_start(
                out=slot.ap[:, 0:8, :].rearrange("p (a b) n -> p a (b n)", a=4),
                in_=wb["pool"][g * 512:(g + 1) * 512, :].rearrange("(k p) n -> p k n", p=128)),
                reads=[dr("wb_pool")], writes=[slot], dma="w_" + slot.name)
            wv = slot.ap[:, 0:8, :].rearrange("p (a b) n -> p a (b n)", a=4)
            for nn in range(4):
                pb = next_ps()

                def f(t, wv=wv, nn=nn, pb=pb, g=g):
                    ins = None
                    for kc in range(4):
                        ins = t.matmul(pb.ap, lhsT=wv[:, kc, nn * 128:(nn + 1) * 128], rhs=v(mixedT)[:, 4 * g + kc, :],
                                       start=(kc == 0), stop=(kc == 3))
                    return ins
                P.op("pe", f, reads=[slot, mixedT], writes=[pb])
                n = 4 * g + nn
                P.op("dve", lambda e, pb=pb, n=n: e.tensor_scalar(out=v(mixT)[:, n, :], in0=pb.ap,
                                                                scalar1=v(pscale)[:, n:n + 1], scalar2=None, op0=ALU.mult),
                     reads=[pb, pscale], writes=[mixT])

        mark(5)
        nkc = (ti + 1) * 4
        kvslots = [(sb("kts0", W0, [NSLOT], BF16), sb("vs0", W0 + 16384, [NSLOT // 128, 128], BF16)),
                   (sb("kts1", B0, [NSLOT], BF16), sb("vs1", B0 + 16384, [NSLOT // 128, 128], BF16))]
        pT = [sb("pT%d" % i, B0 + 49152 + i * 1024, [T], BF16) for i in range(3)]
        maskT = [sb("maskT%d" % i, B0 + 52224 + i * 1024, [T], BF16, parts=32) for i in range(2)]
        rinv = sb("rinv", B0 + 54272, [T])
        scale = 128.0 ** -0.5
        mb8 = [[sb("mb8_%d_%d" % (i, j), E0 + 5120 + (i * 4 + j) * 64, [NBLK], BF16) for j in range(4)] for i in range(2)]

        def load_kv(h):
            kts, vs = kvslots[h % 2]
            P.op("sp", lambda s: s.dma_start(out=v(kts)[:, 0:nkc * 128], in_=kt_scr[h, :, 0:nkc * 128]),
                 reads=[dr("kt_%d" % i) for i in range(ti + 1)], writes=[kts], dma="kv_" + kts.name)
            P.op("sp", lambda s: s.dma_start(
                out=v(vs)[:, 0:nkc, :],
                in_=v_scr[0:nkc * 128, h * 128:(h + 1) * 128].rearrange("(c p) d -> p c d", p=128)),
                reads=[dr("v_%d" % i) for i in range(ti + 1)], writes=[vs], dma="kv_" + vs.name)

        def mask_a(h):
            pg = PB[0]
            for sub in range(4):
                ob = 2 * oi + sub // 2
                mb = mb8[h % 2][sub]
                P.op("pe", lambda t, sub=sub: t.matmul(
                    pg.ap[:, sub * NBLK:(sub + 1) * NBLK], lhsT=v(qT)[:, h, sub * 128:(sub + 1) * 128], rhs=v(kmT)[:, h, :],
                    start=True, stop=True), reads=[qT, kmT], writes=[pg])
                P.op("dve", lambda e, sub=sub, ob=ob: e.tensor_tensor(out=v(g2), in0=pg.ap[:, sub * NBLK:(sub + 1) * NBLK],
                                                                     in1=v(vbg)[:, ob, :], op=ALU.add),
                     reads=[pg, vbg], writes=[g2])
                P.op("dve", lambda e: e.max(out=v(top8), in_=v(g2)), reads=[g2], writes=[top8])
                P.op("dve", lambda e: e.tensor_scalar(out=v(selb), in0=v(g2), scalar1=v(top8)[:, 2:3], scalar2=BIG,
                                                      op0=ALU.is_ge, op1=ALU.mult),
                     reads=[g2, top8], writes=[selb])
                P.op("dve", lambda e, ob=ob: e.tensor_tensor(out=v(selb), in0=v(selb), in1=v(vbgm)[:, ob, :], op=ALU.add),
                     reads=[selb, vbgm], writes=[selb])
                P.op("dve", lambda e, ob=ob, mb=mb: e.tensor_tensor(out=v(mb), in0=v(selb), in1=v(ndg)[:, ob, :],
                                                                   op=ALU.mult),
                     reads=[selb, ndg], writes=[mb])

        def mask_b(h):
            pg = PB[1]
            mT = maskT[h % 2]

            def f(t):
                ins = None
                for sub in range(4):
                    ins = t.transpose(out=pg.ap.bitcast(BF16)[0:NBLK, sub * 128:(sub + 1) * 128], in_=v(mb8[h % 2][sub]),
                                      identity=v(identb))
                return ins
            P.op("pe", f, reads=mb8[h % 2] + [identb], writes=[pg])
            P.op("act", lambda e: e.copy(out=v(mT), in_=pg.ap.bitcast(BF16)[0:NBLK, 0:T]), reads=[pg], writes=[mT])

        load_kv(0)
        mask_a(0)
        mask_b(0)
        if HEADS > 1:
            load_kv(1)
        pos_a = 1
        pos_b = max(2, (3 * nkc) // 4)
        for h in range(HEADS):
            kts, vs = kvslots[h % 2]
            mT = maskT[h % 2]
            psO, psR = (PB[5], PB[6]) if h % 2 == 0 else (PB[7], PB[2])
            sbanks = (3, 4)
            for step in range(nkc + 1):
                ci = step
                if ci < nkc:
                    if h + 1 < HEADS and ci == pos_a:
                        mask_a(h + 1)
                    if h + 1 < HEADS and ci == pos_b:
                        mask_b(h + 1)
                    j = ci // 2
                    pS = PB[sbanks[ci % 2]]
                    pt = pT[ci % 3]

                    def fs(t, pS=pS, ci=ci, j=j, kts=kts, mT=mT, h=h):
                        t.matmul(pS.ap, lhsT=v(kts)[:, ci * 128:(ci + 1) * 128], rhs=v(qT)[:, h, :], start=True, stop=False)
                        return t.matmul(pS.ap, lhsT=v(eblk)[:, j, :], rhs=v(mT), start=False, stop=True)
                    P.op("pe", fs, reads=[kts, qT, eblk, mT], writes=[pS])
                    P.op("act", lambda e, pS=pS, pt=pt: e.activation(out=v(pt), in_=pS.ap, func=AF.Exp, scale=scale),
                         reads=[pS], writes=[pt])
                    cl = ci - ti * 4
                    if cl >= 0:
                        P.op("dve", lambda e, pt=pt, cl=cl: e.tensor_tensor(out=v(pt), in0=v(pt), in1=v(cm)[:, cl, :],
                                                                            op=ALU.mult),
                             reads=[pt, cm], writes=[pt])
                if step >= 1:
                    cj = step - 1
                    ptj = pT[cj % 3]

                    def fo(t, pt=ptj, ci=cj, vs=vs, psO=psO, psR=psR):
                        t.matmul(psO.ap, lhsT=v(vs)[:, ci, :], rhs=v(pt), start=(ci == 0), stop=(ci == nkc - 1))
                        return t.matmul(psR.ap, lhsT=v(ones), rhs=v(pt), start=(ci == 0), stop=(ci == nkc - 1))
                    P.op("pe", fo, reads=[vs, ptj, ones], writes=[psO, psR])
            if h + 2 < HEADS:
                load_kv(h + 2)
            issue_casts(4, [rinv])
            P.op("dve", lambda e, psR=psR: e.reciprocal(out=v(rinv), in_=psR.ap), reads=[psR], writes=[rinv])
            P.op("dve", lambda e, psO=psO, h=h: e.tensor_tensor(out=v(mixT)[:, 16 + h, :], in0=psO.ap, in1=v(rinv),
                                                               op=ALU.mult),
                 reads=[psO, rinv], writes=[mixT])

        mark(6)
        issue_casts(len(cast_q))
        ystB = [sb("ystB%d_" % i, B0 + i * 4096, [4, 256]) for i in range(2)]
        ystA = [sb("ystA%d_" % i, A0 + i * 4096, [4, 256]) for i in range(2)]
        proj_tm("out", mixT, KC, ystB, dst=y_scr, dst_res=dr("y_scr"))
        lnb = ln_bufs()
        hT = actA
        ln_stage(dr("y_scr"), xall[s0:s0 + T, :], dr("xin"), 0, h_scr, dr("h_scr"), hT, lnb)
        mark(7)
        q2T = sb("q2T", B0, [KC, T], BF16)
        sc2 = 1024.0 ** -0.5
        proj_fm("xq", hT, KC, 0, KC, lambda n, pb: copy_op(ev_eng(), v(q2T)[:, n, :], pb.ap, [pb], [q2T]))
        kmTs = sb("kmTs", B0 + 32768, [KC, MEM], BF16)
        vms = sb("vms", B0 + 49152, [2, D], BF16)
        pT2 = [sb("pT2_%d" % i, B0 + 65536 + i * 2048, [2, T], BF16) for i in range(2)]
        rinv2 = sb("rinv2", B0 + 69632, [T])
        P.op("sp", lambda s: s.dma_start(out=v(kmTs), in_=kmT_scr.rearrange("(k p) m -> p k m", p=128)),
             reads=[dr("kmT_scr")], writes=[kmTs], dma="kmTs")
        P.op("sp", lambda s: s.dma_start(out=v(vms), in_=vm_scr.rearrange("(s p) n -> p s n", p=128)),
             reads=[dr("vm_scr")], writes=[vms], dma="vms")
        o2T = actA
        for xh in range(4):
            p2 = pT2[xh % 2]
            for mc in range(2):
                pS = next_ps((3, 4))

                def fs(t, pS=pS, mc=mc, xh=xh):
                    ins = None
                    for kc in range(8):
                        ins = t.matmul(pS.ap, lhsT=v(kmTs)[:, xh * 8 + kc, mc * 128:(mc + 1) * 128],
                                       rhs=v(q2T)[:, xh * 8 + kc, :], start=(kc == 0), stop=(kc == 7))
                    return ins
                P.op("pe", fs, reads=[kmTs, q2T], writes=[pS])
                P.op("act", lambda e, pS=pS, p2=p2, mc=mc: e.activation(out=v(p2)[:, mc, :], in_=pS.ap, func=AF.Exp,
                                                                       scale=sc2),
                     reads=[pS], writes=[p2])
            psR = PB[2]

            def fr(t, p2=p2, psR=psR):
                t.matmul(psR.ap, lhsT=v(ones), rhs=v(p2)[:, 0, :], start=True, stop=False)
                return t.matmul(psR.ap, lhsT=v(ones), rhs=v(p2)[:, 1, :], start=False, stop=True)
            P.op("pe", fr, reads=[ones, p2], writes=[psR])
            P.op("dve", lambda e, psR=psR: e.reciprocal(out=v(rinv2), in_=psR.ap), reads=[psR], writes=[rinv2])
            for dc in range(8):
                psO = next_ps((5, 6, 7))

                def fo(t, p2=p2, psO=psO, dc=dc, xh=xh):
                    c0 = xh * 1024 + dc * 128
                    t.matmul(psO.ap, lhsT=v(vms)[:, 0, c0:c0 + 128], rhs=v(p2)[:, 0, :], start=True, stop=False)
                    return t.matmul(psO.ap, lhsT=v(vms)[:, 1, c0:c0 + 128], rhs=v(p2)[:, 1, :], start=False, stop=True)
                P.op("pe", fo, reads=[vms, p2], writes=[psO])
                P.op("dve", lambda e, psO=psO, dc=dc, xh=xh: e.tensor_tensor(out=v(o2T)[:, xh * 8 + dc, :], in0=psO.ap,
                                                                            in1=v(rinv2), op=ALU.mult),
                     reads=[psO, rinv2], writes=[o2T])
        mark(8)
        proj_tm("xo", o2T, KC, ystB, dst=y_scr, dst_res=dr("y_scr"))
        h2T = actA
        ln_stage(dr("y_scr"), h_scr, dr("h_scr"), 2, h_scr, dr("h_scr"), h2T, lnb)
        mark(9)
        actT = sb("actT", B0, [FC, T], BF16)
        sg = [sb("sg%d" % i, W0 + 32768 + i * 2048, [T]) for i in range(2)]

        need_weight("gate")
        need_weight("up")

        def load_f(name, n, slot):
            src = wb[name][:, n * 128:(n + 1) * 128].rearrange("(k p) n -> p k n", p=128)
            P.op("sp", lambda s: s.dma_start(out=slot.ap, in_=src), reads=[dr("wb_" + name)], writes=[slot],
                 dma="w_" + slot.name)
            return slot
        fl = {}

        def ensf(n):
            if n < FC and n not in fl:
                fl[n] = (load_f("gate", n, fslot[(n % 2) * 2]), load_f("up", n, fslot[(n % 2) * 2 + 1]))
        ensf(0)
        for n in range(FC):
            ensf(n + 1)
            sl_g, sl_u = fl.pop(n)
            pg, pu = next_ps(), next_ps()

            def fg(t, sl=sl_g, pb=pg):
                ins = None
                for kc in range(KC):
                    ins = t.matmul(pb.ap, lhsT=sl.ap[:, kc, 0:128], rhs=v(h2T)[:, kc, :], start=(kc == 0), stop=(kc == KC - 1))
                return ins
            P.op("pe", fg, reads=[sl_g, h2T], writes=[pg])

            def fu(t, sl=sl_u, pb=pu):
                ins = None
                for kc in range(KC):
                    ins = t.matmul(pb.ap, lhsT=sl.ap[:, kc, 0:128], rhs=v(h2T)[:, kc, :], start=(kc == 0), stop=(kc == KC - 1))
                return ins
            P.op("pe", fu, reads=[sl_u, h2T], writes=[pu])
            s_ = sg[n % 2]
            P.op("act", lambda e, s_=s_, pg=pg: e.activation(out=v(s_), in_=pg.ap, func=AF.Silu), reads=[pg], writes=[s_])
            P.op("dve", lambda e, s_=s_, pu=pu, n=n: e.tensor_tensor(out=v(actT)[:, n, :], in0=pu.ap, in1=v(s_), op=ALU.mult),
                 reads=[pu, s_], writes=[actT])
        mark(10)
        proj_tm("down", actT, FC, ystA, dst=y_scr, dst_res=dr("y_scr"))
        lnb = ln_bufs()
        ln_stage(dr("y_scr"), h_scr, dr("h_scr"), 4, out_d[oi * T:(oi + 1) * T, :], dr("out"), None, lnb)

    for ti_ in range(NT):
        if ti_ == nctx:
            mem_stage()
        do_tile(ti_)
        if ti_ < nctx:
            issue_casts(4, [dr("kt_%d" % ti_)])

    P.frozen = False
    P.final_op = P.op("sp", None, reads=[dr("out")])

    with nc.Block() as block:
        P.emit(nc, block, stack)
    stack.close()
    return nc


def _host_tables(qt, nctx=12, nown=4):
    NSLOT = (nctx + nown) * T
    nctx_tok = nctx * T
    nvalid_tok = min(qt * nown * T, nctx_tok)
    pos = np.zeros(NSLOT, np.int64)
    pos[nctx_tok - nvalid_tok:nctx_tok] = np.arange(nvalid_tok) + (qt * nown * T - nvalid_tok)
    pos[nctx_tok:] = qt * nown * T + np.arange(nown * T)
    half = 64
    inv = (np.float32(10000.0) ** (-np.arange(half, dtype=np.float32) / np.float32(half))).astype(np.float32)
    ang = pos.astype(np.float32)[:, None] * inv[None, :]
    cos = np.cos(ang).astype(np.float32).T
    sin = np.sin(ang).astype(np.float32).T
    cosT = np.ascontiguousarray(np.concatenate([cos, cos], 0))
    sinT = np.ascontiguousarray(np.concatenate([sin, sin], 0))
    gpos = qt * nown * T + np.arange(nown * T)
    invc = np.stack([1.0 / np.minimum(gpos + 1, w) for w in WINS]).astype(np.float32)
    invc = np.ascontiguousarray(np.broadcast_to(invc[None], (128, 4, nown * T)))
    nvalid_blk = nvalid_tok // 256
    vbg = np.full((8, NBLK), -BIG, np.float32)
    ndg = np.ones((8, NBLK), np.float32)
    for ob in range(2 * nown):
        vbg[ob, 2 * nctx - nvalid_blk:2 * nctx] = 0.0
        vbg[ob, 2 * nctx:2 * nctx + ob] = 0.0
        ndg[ob, 2 * nctx + ob] = 0.0
    vbg = np.ascontiguousarray(np.broadcast_to(vbg[None], (128, 8, NBLK)))
    ndg = np.ascontiguousarray(np.broadcast_to(ndg[None], (128, 8, NBLK)))
    return cosT, sinT, invc, vbg, ndg


def _consts():
    k = np.arange(128)[:, None]
    cmk = np.zeros((128, 4, T), np.float32)
    for cl in range(4):
        cmk[:, cl, :] = ((cl * 128 + k) <= np.arange(T)[None, :]).astype(np.float32)
    eb = np.zeros((32, NBLK, 128), np.float32)
    for j in range(NBLK):
        eb[j, j, :] = 1.0
    psw = np.zeros((128, 128), np.float32)
    for m in range(64):
        psw[m + 64, m] = -1.0
        psw[m, m + 64] = 1.0
    return cmk, eb, psw, np.eye(128, dtype=np.float32)


_NC_CACHE = {}


def kernel(x, mem, w_mix_in, w_pool, pool_scale, w_mix_out, ln1_g, ln1_b, w_xq, w_xkv, w_xo,
           ln2_g, ln2_b, w_gate, w_up, w_down, ln3_g, ln3_b):
    x = np.asarray(x, np.float32)
    mem = np.asarray(mem, np.float32)
    B, S, _ = x.shape
    nq = 4
    own = S // nq
    if "nc" not in _NC_CACHE:
        _NC_CACHE["nc"] = build()
    nc = _NC_CACHE["nc"]
    cmk, eb, psw, idn = _consts()
    lnp = np.ascontiguousarray(np.stack([np.asarray(a, np.float32)[0] for a in (ln1_g, ln1_b, ln2_g, ln2_b, ln3_g, ln3_b)]))
    psc = np.ascontiguousarray(np.asarray(pool_scale, np.float32)[0].reshape(16, 128).T)
    shared = {
        "w_in": np.asarray(w_mix_in, np.float32)[0], "w_pool": np.asarray(w_pool, np.float32)[0].reshape(2048, 512),
        "w_out": np.asarray(w_mix_out, np.float32)[0], "w_xq": np.asarray(w_xq, np.float32)[0],
        "w_xkv": np.asarray(w_xkv, np.float32)[0], "w_xo": np.asarray(w_xo, np.float32)[0],
        "w_gate": np.asarray(w_gate, np.float32)[0], "w_up": np.asarray(w_up, np.float32)[0],
        "w_down": np.asarray(w_down, np.float32)[0], "lnp": lnp, "pscale": psc,
        "cm": cmk, "eblk": eb, "pswap": psw, "ident": idn,
    }
    in_maps = []
    for c in range(8):
        b, qt = c // nq, c % nq
        xall = np.zeros((3 * own + own, D), np.float32)
        xall[3 * own - qt * own:3 * own] = x[b, :qt * own]
        xall[3 * own:] = x[b, qt * own:(qt + 1) * own]
        cosT, sinT, invc, vbg, ndg = _host_tables(qt)
        m = dict(shared)
        m.update({"xall": xall, "mem": mem[b], "cosT": cosT, "sinT": sinT, "invc": invc, "vbg": vbg, "ndg": ndg})
        in_maps.append(m)
    res = run_bass_kernel_spmd(nc, in_maps, core_ids=list(range(8)))
    out = np.empty((B, S, D), np.float32)
    for c in range(8):
        b, qt = c // nq, c % nq
        out[b, qt * own:(qt + 1) * own] = np.asarray(res.results[c]["out"], np.float32)
    return out
```

```python
import os
import numpy as np
import concourse.bass as bass
import concourse.mybir as mybir
from concourse.bass_utils import run_bass_kernel_spmd

F32, BF16 = mybir.dt.float32, mybir.dt.bfloat16
AF = mybir.ActivationFunctionType
ALU = mybir.AluOpType
AX = mybir.AxisListType

D = 4096
KC = 32
T = 512
HEADS = 16
FFN = 11008
FC = 86
MEM = 256
BIG = 30000.0
ALPHA = 2.0 ** 0.25
EPS = 1e-5
NBLK = 32
WINS = (2, 4, 8, 16)


class Buf:
    __slots__ = ("name", "space", "lo", "hi", "ap", "overl", "last_w", "rd", "rd_dma")

    def __init__(self, name, space, lo, hi, ap):
        self.name, self.space, self.lo, self.hi, self.ap = name, space, lo, hi, ap
        self.overl = [self]
        self.last_w = None
        self.rd = {}
        self.rd_dma = []


class Op:
    __slots__ = ("eng", "fn", "deps", "dma", "cnt", "need_inc", "idx")


def _semname(n):
    if n.startswith("c_"):
        return "const"
    if n == "cast_in_kv":
        return "castA"
    if n in ("cast_in_pq", "cast_pool"):
        return "castB"
    if n == "cast_xkv":
        return "castC"
    if n.startswith("cast_"):
        return "castD"
    if n in ("cos", "sin"):
        return "cs"
    if n.startswith("kv_"):
        return "kv" + n[-1]
    if n.startswith("w_wslot"):
        return "w" + n[7]
    if n.startswith("w_fslot"):
        return "f" + str(int(n[7]) // 2)
    if n.startswith("yst_"):
        return "yst" + n.rstrip("_")[-1]
    if n in ("ln_g", "ln_b"):
        return "lngb"
    if n in ("kmTs", "vms"):
        return "xkv"
    if n.startswith("st_"):
        return "st" + n[-1]
    return n


class Prog:
    ENGS = ("pe", "act", "dve", "pool", "sp")

    def __init__(self):
        self.ops = {e: [] for e in self.ENGS}
        self.bufs = {}
        self.dma_cnt = {}
        self.nops = 0

    def buf(self, name, space, lo, nbytes, ap):
        b = Buf(name, space, lo, lo + nbytes, ap)
        lst = self.bufs.setdefault(space, [])
        for o in lst:
            if o.lo < b.hi and b.lo < o.hi:
                o.overl.append(b)
                b.overl.append(o)
        lst.append(b)
        return b

    frozen = False

    def op(self, eng, fn, reads=(), writes=(), dma=None):
        if self.frozen:
            return None
        if dma is not None:
            dma = _semname(dma)
        o = Op()
        o.eng, o.fn, o.dma, o.need_inc, o.cnt = eng, fn, dma, False, 0
        o.idx = self.nops
        self.nops += 1
        deps = {}

        def add(d):
            if d is None:
                return
            if d.dma is None and d.eng == "pe" and eng == "pe" and dma is None:
                return
            if d.dma is not None:
                deps[id(d)] = (d, self.dma_cnt[d.dma])
            else:
                deps[id(d)] = (d, None)

        for b in reads:
            for ob in b.overl:
                add(ob.last_w)
                if b.space == "ps":
                    for r in ob.rd.values():
                        if r.eng != eng:
                            add(r)
        for b in writes:
            for ob in b.overl:
                add(ob.last_w)
                for r in ob.rd.values():
                    add(r)
                for r in ob.rd_dma:
                    add(r)
        o.deps = list(deps.values())
        for d, _ in o.deps:
            d.need_inc = True
        for b in writes:
            b.last_w = o
            b.rd = {}
            b.rd_dma = []
        for b in reads:
            if dma is not None:
                b.rd_dma.append(o)
            else:
                b.rd[eng] = o
        if dma is not None:
            self.dma_cnt[dma] = self.dma_cnt.get(dma, 0) + 16
            o.cnt = self.dma_cnt[dma]
        self.ops[eng].append(o)
        return o

    def emit(self, nc, block, stack):
        sems = {}

        def sem(name):
            if name not in sems:
                sems[name] = stack.enter_context(nc.semaphore("s_" + name))
            return sems[name]

        cnt = {e: 0 for e in self.ENGS}
        allops = sorted((o for e in self.ENGS for o in self.ops[e]), key=lambda o: o.idx)
        for o in allops:
            if o.dma is None and o.need_inc:
                cnt[o.eng] += 1
                o.cnt = cnt[o.eng]
        for e in self.ENGS:
            sem("e_" + e)
        for name in self.dma_cnt:
            sem("d_" + name)

        final_op = self.final_op

        def run(eng_name):
            def body(e):
                waited = {}
                for o in self.ops[eng_name]:
                    need = {}
                    for d, v in o.deps:
                        if d.dma is not None:
                            k, val = "d_" + d.dma, v
                        else:
                            k, val = "e_" + d.eng, d.cnt
                        if val > need.get(k, 0):
                            need[k] = val
                    if o.dma is not None and o.cnt > 16:
                        k = "d_" + o.dma
                        need[k] = max(need.get(k, 0), o.cnt - 16)
                    wl = []
                    for k, val in need.items():
                        if waited.get(k, 0) < val:
                            e.wait_ge(sems[k], val)
                            waited[k] = val
                            wl.append((k, val))
                    if os.environ.get("KDUMP"):
                        print("OP", eng_name, o.idx, "waits", wl, "inc", (o.dma, o.cnt) if o.dma else (o.cnt if o.need_inc else None),
                              getattr(o, "tag", ""))
                    if o.fn is None:
                        if o is final_op:
                            for nm, tot in self.dma_cnt.items():
                                e.wait_ge(sems["d_" + nm], tot)
                        continue
                    ins = o.fn(e)
                    if o.dma is not None:
                        ins.then_inc(sems["d_" + o.dma], 16)
                    elif o.need_inc:
                        ins.then_inc(sems["e_" + eng_name], 1)
            return body

        block.tensor(run("pe"))
        block.scalar(run("act"))
        block.vector(run("dve"))
        block.gpsimd(run("pool"))
        block.sync(run("sp"))


def build(nctx=12, nown=4, debug=False):
    nc = bass.Bass("TRN2", target_bir_lowering=False)
    NT = nctx + nown
    NSLOT = NT * T
    CTXB = nctx * 2

    def din(name, shape, dt=F32):
        return nc.dram_tensor(name, shape, dt, kind="ExternalInput").ap()

    def dscr(name, shape, dt):
        return nc.dram_tensor(name, shape, dt, kind="Internal").ap()

    xall = din("xall", [NSLOT, D])
    mem = din("mem", [MEM, D])
    w_in = din("w_in", [D, 2 * D])
    w_pool = din("w_pool", [4 * 512, 512])
    w_out = din("w_out", [D, D])
    w_xq = din("w_xq", [D, D])
    w_xkv = din("w_xkv", [D, 2 * D])
    w_xo = din("w_xo", [D, D])
    w_gate = din("w_gate", [D, FFN])
    w_up = din("w_up", [D, FFN])
    w_down = din("w_down", [FFN, D])
    lnp = din("lnp", [6, D])
    pscale_d = din("pscale", [128, 16])
    cos_d = din("cosT", [128, NSLOT])
    sin_d = din("sinT", [128, NSLOT])
    invc_d = din("invc", [128, 4, nown * T])
    vbg_d = din("vbg", [128, 8, NBLK])
    ndg_d = din("ndg", [128, 8, NBLK])
    cm_d = din("cm", [128, 4, T])
    eb_d = din("eblk", [32, NBLK, 128])
    psw_d = din("pswap", [128, 128])
    idn_d = din("ident", [128, 128])
    out_d = nc.dram_tensor("out", [nown * T, D], F32, kind="ExternalOutput").ap()

    wb = {
        "in": dscr("wb_in", [D, 2 * D], BF16), "pool": dscr("wb_pool", [2048, 512], BF16),
        "out": dscr("wb_out", [D, D], BF16), "xq": dscr("wb_xq", [D, D], BF16),
        "xkv": dscr("wb_xkv", [D, 2 * D], BF16), "xo": dscr("wb_xo", [D, D], BF16),
        "gate": dscr("wb_gate", [D, FFN], BF16), "up": dscr("wb_up", [D, FFN], BF16),
        "down": dscr("wb_down", [FFN, D], BF16),
    }
    wsrc = {"in": w_in, "pool": w_pool, "out": w_out, "xq": w_xq, "xkv": w_xkv, "xo": w_xo,
            "gate": w_gate, "up": w_up, "down": w_down}
    kt_scr = dscr("kt_scr", [HEADS, 128, NSLOT], BF16)
    v_scr = dscr("v_scr", [NSLOT, 2048], BF16)
    kmT_scr = dscr("kmT_scr", [D, MEM], BF16)
    vm_scr = dscr("vm_scr", [MEM, D], BF16)
    y_scr = dscr("y_scr", [T, D], F32)
    h_scr = dscr("h_scr", [T, D], F32)

    P = Prog()
    import os
    STOP = int(os.environ.get("KSTOP", "99"))

    def mark(n):
        if STOP == n:
            P.frozen = True
    import contextlib
    stack = contextlib.ExitStack()
    ARENA = 204800
    E0 = 190464
    arena = stack.enter_context(nc.sbuf_tensor("arena", [128, ARENA // 4], F32))
    psum = stack.enter_context(nc.psum_tensor("psum", [128, 8, 512], F32))

    def sb(name, off, shape, dt=F32, parts=128):
        esz = 4 if dt == F32 else 2
        n = int(np.prod(shape))
        nbytes = n * esz
        assert off % 4 == 0 and off + nbytes <= ARENA, (name, off, nbytes)
        ap = arena[0:parts, off // 4:(off + nbytes + 3) // 4]
        if dt != F32:
            ap = ap.bitcast(dt)
        if len(shape) == 2:
            ap = ap.rearrange("p (a b) -> p a b", a=shape[0])
        elif len(shape) == 3:
            ap = ap.rearrange("p (a b c) -> p a b c", a=shape[0], b=shape[1])
        return P.buf(name, "sb", off, nbytes, ap)

    PB = [P.buf("ps%d" % i, "ps", i * 2048, 2048, psum[:, i, :]) for i in range(8)]
    dres = {}

    def dr(name):
        if name not in dres:
            dres[name] = P.buf(name, "dram:" + name, 0, 1, None)
        return dres[name]

    W0, W1, A0, B0, C0 = 0, 22528, 45056, 77824, 165888
    WS = 22528
    c = C0
    identf = sb("identf", c, [128]); c += 512
    identb = sb("identb", c, [128], BF16); c += 256
    pswap = sb("pswap", c, [128], BF16); c += 256
    ones = sb("ones", c, [128], BF16); c += 256
    eblk = sb("eblk", c, [NBLK, 128], BF16, parts=32); c += 8192
    cm = sb("cm", c, [4, T], BF16); c += 4096
    kmsum = sb("kmsum", c, [HEADS, NBLK]); c += 2048
    kmT = sb("kmT", c, [HEADS, NBLK], BF16); c += 1024
    vbg = sb("vbg", c, [8, NBLK]); c += 1024
    vbgm = sb("vbgm", c, [8, NBLK]); c += 1024
    ndg = sb("ndg", c, [8, NBLK]); c += 1024
    halo = sb("halo", c, [16, 16]); c += 1024
    pscale = sb("pscale", c, [16]); c += 64
    epsc = sb("epsc", c, [1]); c += 4
    g2 = sb("g2", c, [NBLK]); c += 128
    top8 = sb("top8", c, [8]); c += 32
    selb = sb("selb", c, [NBLK]); c += 128
    mbias = [sb("mbias%d" % i, c + 128 * i, [NBLK], BF16) for i in range(2)]; c += 256
    stats = sb("stats", c, [8, 6]); c += 192
    mv = sb("mv", c, [2]); c += 8
    rstd = sb("rstd", c, [1]); c += 4
    ksum2 = sb("ksum2", c, [2]); c += 8
    assert c <= ARENA, c

    wslot_l = [sb("wslot0", W0, [43, 256], BF16), sb("wslot1", W1, [43, 256], BF16)]
    wslot_s = [sb("wslot0s", W0, [32, 256], BF16), sb("wslot1s", W1, [32, 256], BF16)]
    fslot = [sb("fslot%d" % i, W0 + i * 8192, [32, 128], BF16) for i in range(4)]
    actA = sb("actA", A0, [KC, T], BF16)

    def v(b):
        return b.ap

    def cast_weight(name, c0=None, c1=None, tag=None):
        src, dst = wsrc[name], wb[name]
        rows = src.shape[0]
        if c0 is None:
            c0, c1 = 0, src.shape[1]
        tag = tag or name
        step = max(128, (8 << 20) // ((c1 - c0) * 4) // 128 * 128)
        r = 0
        while r < rows:
            e = min(rows, r + step)
            P.op("pool", (lambda g, r=r, e=e: g.dma_start(out=dst[r:e, c0:c1], in_=src[r:e, c0:c1])),
                 writes=[dr("wb_" + tag)], dma="cast_" + tag)
            r = e


    def load_const(dst, src_ap, shape, parts=128, conv=True, off=0):
        nel = int(np.prod(shape))
        st = sb("cst_%s" % dst.name, B0 + off, shape, F32, parts=parts)
        P.op("sp", lambda s: s.dma_start(out=v(st), in_=src_ap), writes=[st], dma="c_" + dst.name)
        conv_list.append((dst, st))
        return nel * 4

    o = 0
    conv_list = []
    P.op("sp", lambda s: s.dma_start(out=v(identf), in_=idn_d[:, :]), writes=[identf], dma="c_ident")
    o += load_const(pswap, psw_d[:, :], [128], off=o)
    o += load_const(eblk, eb_d[:, :, :], [NBLK, 128], parts=32, off=o)
    o += load_const(cm, cm_d[:, :, :], [4, T], off=o)
    P.op("sp", lambda s: s.dma_start(out=v(vbg), in_=vbg_d[:, :, :]), writes=[vbg], dma="c_vbg")
    P.op("sp", lambda s: s.dma_start(out=v(ndg), in_=ndg_d[:, :, :]), writes=[ndg], dma="c_ndg")
    P.op("sp", lambda s: s.dma_start(out=v(pscale), in_=pscale_d[:, :]), writes=[pscale], dma="c_pscale")
    P.op("dve", lambda e: e.tensor_copy(out=v(identb), in_=v(identf)), reads=[identf], writes=[identb])
    for dst_, st_ in conv_list:
        P.op("dve", lambda e, dst_=dst_, st_=st_: e.tensor_copy(out=v(dst_), in_=v(st_)), reads=[st_], writes=[dst_])
    P.op("dve", lambda e: e.tensor_scalar(out=v(vbgm), in0=v(vbg), scalar1=-BIG, scalar2=None, op0=ALU.add),
         reads=[vbg], writes=[vbgm])
    P.op("dve", lambda e: e.memset(v(ones), 1.0), writes=[ones])
    P.op("dve", lambda e: e.memset(v(epsc), EPS), writes=[epsc])
    P.op("dve", lambda e: e.memset(v(halo), 0.0), writes=[halo])
    P.op("dve", lambda e: e.memset(v(kmsum), 0.0), writes=[kmsum])
    P.op("dve", lambda e: e.memset(v(kmT), 0.0), writes=[kmT])

    cast_weight("in", 4096, 8192, "in_kv")
    cast_weight("in", 0, 4096, "in_pq")
    for name in ("pool", "xkv", "out", "xq", "xo", "gate", "up", "down"):
        cast_weight(name)

    mark(1)
    psrr = [0]

    def next_ps(banks=(2, 3, 4, 5, 6, 7)):
        b = banks[psrr[0] % len(banks)]
        psrr[0] += 1
        return PB[b]

    evrr = [0]

    def ev_eng():
        evrr[0] += 1
        return "act" if evrr[0] % 2 else "dve"

    def copy_op(eng, dst_ap, src_ap, reads, writes):
        if eng == "act":
            P.op("act", lambda e: e.copy(out=dst_ap, in_=src_ap), reads=reads, writes=writes)
        else:
            P.op(eng, lambda e: e.tensor_copy(out=dst_ap, in_=src_ap), reads=reads, writes=writes)

    def transpose_rows(src_buf, src_ap, dstT, col0, ncols=128):
        for k4 in range(KC // 4):
            pb = PB[k4 % 2]

            def f(t, k4=k4, pb=pb):
                ins = None
                for j in range(4):
                    kc = k4 * 4 + j
                    ins = t.transpose(out=pb.ap[:, j * 128:j * 128 + ncols],
                                      in_=src_ap[0:ncols, kc * 128:(kc + 1) * 128],
                                      identity=v(identf)[0:ncols, 0:ncols])
                return ins
            P.op("pe", f, reads=[src_buf, identf], writes=[pb])
            copy_op(ev_eng(), v(dstT)[:, k4 * 4:(k4 + 1) * 4, col0:col0 + ncols],
                    pb.ap.rearrange("p (a b) -> p a b", a=4)[:, :, 0:ncols], [pb], [dstT])

    wrr = [0]

    def load_panel(name, k0, kn, c0, cn):
        slot = (wslot_s if kn <= 32 else wslot_l)[wrr[0] % 2]
        wrr[0] += 1
        src = wb[name][k0 * 128:(k0 + kn) * 128, c0:c0 + cn].rearrange("(k p) n -> p k n", p=128)
        rname = name if name != "in" else ("in_kv" if c0 >= 4096 else "in_pq")
        P.op("sp", lambda s: s.dma_start(out=slot.ap[:, 0:kn, 0:cn], in_=src),
             reads=[dr("wb_" + rname)], writes=[slot], dma="w_" + slot.name)
        return slot

    def proj_fm(name, actT, kn, c0, nchunks, consume, ntok=T, pair=None):
        n = 0
        pend = None
        panels = []
        for p0 in range(0, nchunks, 2):
            cn = min(2, nchunks - p0)
            panels.append((p0, cn))
        loaded = {}

        def ensure(i):
            if i < len(panels) and i not in loaded:
                p0, cn = panels[i]
                loaded[i] = load_panel(name, 0, kn, c0 + p0 * 128, cn * 128)
        ensure(0)
        for i, (p0, cn) in enumerate(panels):
            ensure(i + 1)
            slot = loaded.pop(i)
            for j in range(cn):
                pb = next_ps()

                def f(t, slot=slot, j=j, pb=pb):
                    ins = None
                    for kc in range(kn):
                        ins = t.matmul(pb.ap[:, 0:ntok], lhsT=slot.ap[:, kc, j * 128:(j + 1) * 128],
                                       rhs=actT.ap[:, kc, 0:ntok], start=(kc == 0), stop=(kc == kn - 1))
                    return ins
                P.op("pe", f, reads=[slot, actT], writes=[pb])
                consume(p0 + j, pb)

    def proj_tm(name, actT, kn, ystage, nsub=4, dst=None, dst_res=None, ncols=D, dst_bf16=None):
        halves = [(0, kn)] if kn <= 43 else [(0, 43), (43, kn - 43)]
        npan = ncols // 256
        seq = [(p, h) for p in range(npan) for h in range(len(halves))]
        loaded = {}

        def ensure(i):
            if i < len(seq) and i not in loaded:
                p, h = seq[i]
                loaded[i] = load_panel(name, halves[h][0], halves[h][1], p * 256, 256)
        ensure(0)
        si = 0
        two = len(halves) > 1
        for p in range(npan):
            pbs = [next_ps() for _ in range(nsub if two else (nsub + 1) // 2)]
            for h in range(len(halves)):
                ensure(si + 1)
                slot = loaded.pop(si)
                si += 1
                k0, kk = halves[h]

                def f(t, slot=slot, k0=k0, kk=kk, h=h, pbs=pbs):
                    ins = None
                    for sub in range(nsub):
                        if two:
                            out = pbs[sub].ap[:, 0:256]
                        else:
                            out = pbs[sub // 2].ap[:, (sub % 2) * 256:(sub % 2) * 256 + 256]
                        for kc in range(kk):
                            ins = t.matmul(out, lhsT=actT.ap[:, k0 + kc, sub * 128:(sub + 1) * 128],
                                           rhs=slot.ap[:, kc, 0:256],
                                           start=(h == 0 and kc == 0),
                                           stop=(h == len(halves) - 1 and kc == kk - 1))
                    return ins
                P.op("pe", f, reads=[slot, actT], writes=pbs)
            ys = ystage[p % 2]
            if two:
                for sub in range(nsub):
                    copy_op(ev_eng(), ys.ap[:, sub, :], pbs[sub].ap[:, 0:256], [pbs[sub]], [ys])
            else:
                for q in range((nsub + 1) // 2):
                    ns = min(2, nsub - 2 * q)
                    copy_op(ev_eng(), ys.ap[:, 2 * q:2 * q + ns, :],
                            pbs[q].ap.rearrange("p (a b) -> p a b", a=2)[:, 0:ns, :], [pbs[q]], [ys])
            dd = dst[0:nsub * 128, p * 256:(p + 1) * 256].rearrange("(s p) n -> p s n", p=128)
            P.op("sp", lambda s, ys=ys, dd=dd: s.dma_start(out=dd, in_=ys.ap[:, 0:nsub, :]),
                 reads=[ys], writes=[dst_res], dma="yst_" + ys.name)

    def ln_stage(y_res, resid_ap, resid_res, gi, out_ap, out_res, outT, lnb, nsub=4):
        ybufs, rbufs, gbuf, bbuf = lnb
        P.op("sp", lambda s: s.dma_start(out=v(gbuf), in_=lnp[gi, :].partition_broadcast(128)),
             writes=[gbuf], dma="ln_g")
        P.op("sp", lambda s: s.dma_start(out=v(bbuf), in_=lnp[gi + 1, :].partition_broadcast(128)),
             writes=[bbuf], dma="ln_b")

        def load(sub):
            ybuf, rbuf = ybufs[sub % 2], rbufs[sub % 2]
            rs = slice(sub * 128, (sub + 1) * 128)
            P.op("sp", lambda s: s.dma_start(out=v(ybuf), in_=y_scr[rs, :]),
                 reads=[y_res], writes=[ybuf], dma="ln_y%d" % (sub % 2))
            P.op("sp", lambda s: s.dma_start(out=v(rbuf), in_=resid_ap[rs, :]),
                 reads=[resid_res], writes=[rbuf], dma="ln_r%d" % (sub % 2))

        def stage1(sub):
            ybuf, rbuf = ybufs[sub % 2], rbufs[sub % 2]
            P.op("dve", lambda e: e.scalar_tensor_tensor(out=v(ybuf), in0=v(rbuf), scalar=ALPHA, in1=v(ybuf),
                                                         op0=ALU.mult, op1=ALU.add),
                 reads=[rbuf, ybuf], writes=[ybuf])

            def st(e):
                ins = None
                for i in range(8):
                    ins = e.bn_stats(out=v(stats)[:, i, :], in_=v(ybuf)[:, i * 512:(i + 1) * 512])
                return ins
            P.op("dve", st, reads=[ybuf], writes=[stats])
            P.op("dve", lambda e: e.bn_aggr(out=v(mv), in_=v(stats).rearrange("p a b -> p (a b)")),
                 reads=[stats], writes=[mv])
            P.op("act", lambda e: e.activation(out=v(rstd), in_=v(mv)[:, 1:2], func=AF.Sqrt, bias=v(epsc), scale=1.0),
                 reads=[mv, epsc], writes=[rstd])
            P.op("dve", lambda e: e.reciprocal(out=v(rstd), in_=v(rstd)), reads=[rstd], writes=[rstd])
            P.op("dve", lambda e: e.tensor_scalar(out=v(ybuf), in0=v(ybuf), scalar1=v(mv)[:, 0:1], scalar2=v(rstd),
                                                  op0=ALU.subtract, op1=ALU.mult),
                 reads=[ybuf, mv, rstd], writes=[ybuf])

        def stage2(sub):
            ybuf = ybufs[sub % 2]
            rs = slice(sub * 128, (sub + 1) * 128)
            P.op("pool", lambda e: e.tensor_tensor(out=v(ybuf), in0=v(ybuf), in1=v(gbuf), op=ALU.mult),
                 reads=[ybuf, gbuf], writes=[ybuf])
            P.op("dve", lambda e: e.tensor_tensor(out=v(ybuf), in0=v(ybuf), in1=v(bbuf), op=ALU.add),
                 reads=[ybuf, bbuf], writes=[ybuf])
            P.op("sp", lambda s: s.dma_start(out=out_ap[rs, :], in_=v(ybuf)),
                 reads=[ybuf], writes=[out_res], dma="ln_o")
            if outT is not None:
                transpose_rows(ybuf, v(ybuf), outT, sub * 128)

        load(0)
        if nsub > 1:
            load(1)
        stage1(0)
        for sub in range(nsub):
            if sub + 1 < nsub:
                stage1(sub + 1)
            stage2(sub)
            if sub + 2 < nsub:
                load(sub + 2)

    def ln_bufs():
        return ([sb("ln_y0", B0, [D]), sb("ln_y1", B0 + 32768, [D])],
                [sb("ln_r0", B0 + 16384, [D]), sb("ln_r1", B0 + 49152, [D])],
                sb("ln_g", B0 + 65536, [D]), sb("ln_b", W0, [D]))

    def mem_stage():
        memst = sb("memst", B0, [D])
        memT = sb("memT", A0, [KC, T], BF16)
        for sub in range(2):
            P.op("sp", lambda s, sub=sub: s.dma_start(out=v(memst), in_=mem[sub * 128:(sub + 1) * 128, :]),
                 writes=[memst], dma="xst0")
            transpose_rows(memst, v(memst), memT, sub * 128)
        kmstage = [sb("kmstage%d" % i, B0 + 16384 + i * 512, [MEM], BF16) for i in range(2)]

        def km_consume(n, pb):
            ks = kmstage[n % 2]
            copy_op(ev_eng(), v(ks), pb.ap[:, 0:MEM], [pb], [ks])
            P.op("sp", lambda s, ks=ks, n=n: s.dma_start(out=kmT_scr[n * 128:(n + 1) * 128, :], in_=v(ks)),
                 reads=[ks], writes=[dr("kmT_scr")], dma="st_" + ks.name)
        proj_fm("xkv", memT, KC, 0, KC, km_consume, ntok=MEM)
        ystage_b = [sb("ystB%d" % i, B0 + 20480 + i * 4096, [4, 256]) for i in range(2)]
        vmst = [sb("vmst%d" % i, B0 + 32768 + i * 1024, [2, 256], BF16) for i in range(2)]
        npan = D // 256
        loaded = {}

        def ens(i):
            if i < npan and i not in loaded:
                loaded[i] = load_panel("xkv", 0, KC, D + i * 256, 256)
        ens(0)
        for p in range(npan):
            ens(p + 1)
            slot = loaded.pop(p)
            pb = next_ps()

            def f(t, slot=slot, pb=pb):
                ins = None
                for sub in range(2):
                    for kc in range(KC):
                        ins = t.matmul(pb.ap[:, sub * 256:(sub + 1) * 256], lhsT=memT.ap[:, kc, sub * 128:(sub + 1) * 128],
                                       rhs=slot.ap[:, kc, 0:256], start=(kc == 0), stop=(kc == KC - 1))
                return ins
            P.op("pe", f, reads=[slot, memT], writes=[pb])
            vs = vmst[p % 2]
            copy_op(ev_eng(), v(vs), pb.ap.rearrange("p (a b) -> p a b", a=2), [pb], [vs])
            P.op("sp", lambda s, vs=vs, p=p: s.dma_start(
                out=vm_scr[:, p * 256:(p + 1) * 256].rearrange("(s p) n -> p s n", p=128), in_=v(vs)),
                reads=[vs], writes=[dr("vm_scr")], dma="st_" + vs.name)


    mark(2)
    xst = [sb("xst0", B0, [D]), sb("xst1", B0 + 16384, [D])]
    vst = sb("vst", B0, [4, 2048], BF16)
    mixedT = sb("mixedT", B0 + 16384, [16, T], BF16)
    qT = sb("qT", B0 + 32768, [HEADS, T], BF16)
    kst = sb("kst", B0 + 49152, [HEADS, T], BF16)
    cosb = sb("cosb", B0 + 65536, [T])
    sinb = sb("sinb", B0 + 67584, [T])
    invc = sb("invc", B0 + 69632, [4, T])
    qf = sb("qf", B0 + 77824, [T])
    qb = sb("qb", B0 + 79872, [T], BF16)
    t1 = sb("t1", B0 + 80896, [T])
    ropeset = [(qf, qb, t1), (sb("qf2", E0, [T]), sb("qb2", E0 + 2048, [T], BF16), sb("t12", E0 + 3072, [T]))]
    rrr = [0]
    hp = [sb("hp%d" % i, B0 + 82944 + i * 2112, [528]) for i in range(2)]

    def rope_consume(pb, dst_ap, dst_buf, ksum_h=None, blk0=None):
        qf, qb, t1 = ropeset[rrr[0] % 2]
        sw = PB[rrr[0] % 2]
        rrr[0] += 1
        P.op("act", lambda e: e.copy(out=v(qf), in_=pb.ap), reads=[pb], writes=[qf])
        P.op("dve", lambda e: e.tensor_copy(out=v(qb), in_=v(qf)), reads=[qf], writes=[qb])
        P.op("pe", lambda t: t.matmul(sw.ap, lhsT=v(pswap), rhs=v(qb), start=True, stop=True),
             reads=[pswap, qb], writes=[sw])
        P.op("dve", lambda e: e.tensor_tensor(out=v(t1), in0=sw.ap, in1=v(sinb), op=ALU.mult),
             reads=[sw, sinb], writes=[t1])
        P.op("dve", lambda e: e.tensor_tensor(out=v(qf), in0=v(qf), in1=v(cosb), op=ALU.mult),
             reads=[qf, cosb], writes=[qf])
        P.op("dve", lambda e: e.tensor_tensor(out=v(qf), in0=v(qf), in1=v(t1), op=ALU.add),
             reads=[qf, t1], writes=[qf])
        P.op("act", lambda e: e.copy(out=dst_ap, in_=v(qf)), reads=[qf], writes=[dst_buf])
        if ksum_h is not None:
            P.op("dve", lambda e: e.tensor_reduce(out=v(kmsum)[:, ksum_h, blk0:blk0 + 2],
                                                  in_=v(qf).rearrange("p (a b) -> p a b", a=2),
                                                  axis=AX.X, op=ALU.add),
                 reads=[qf], writes=[kmsum])

    def pool_chunk(ch, pb, save_only):
        h = hp[ch % 2]
        g = ch // 4
        w = WINS[g]
        P.op("act", lambda e: e.copy(out=v(h)[:, 16:528], in_=pb.ap), reads=[pb], writes=[h])
        P.op("pool", lambda e: e.tensor_copy(out=v(h)[:, 0:16], in_=v(halo)[:, ch, :]), reads=[halo], writes=[h])
        P.op("pool", lambda e: e.tensor_copy(out=v(halo)[:, ch, :], in_=v(h)[:, 512:528]), reads=[h], writes=[halo])
        if save_only:
            return
        sA, sB = pooltmp[ch % 2]
        P.op("dve", lambda e: e.tensor_tensor(out=v(sA)[:, 1:528], in0=v(h)[:, 1:528], in1=v(h)[:, 0:527], op=ALU.add),
             reads=[h], writes=[sA])
        cur, oth, width, lo = sA, sB, 2, 1
        while width < w:
            lo2 = lo + width
            P.op("dve", lambda e, cur=cur, oth=oth, width=width, lo2=lo2: e.tensor_tensor(
                out=v(oth)[:, lo2:528], in0=v(cur)[:, lo2:528], in1=v(cur)[:, lo2 - width:528 - width], op=ALU.add),
                reads=[cur], writes=[oth])
            cur, oth = oth, cur
            width *= 2
            lo = lo2
        assert lo <= 16
        P.op("pool", lambda e, cur=cur: e.tensor_tensor(out=v(cur)[:, 16:528], in0=v(cur)[:, 16:528],
                                                       in1=v(invc)[:, g, :], op=ALU.mult),
             reads=[cur, invc], writes=[cur])
        P.op("dve", lambda e, cur=cur: e.tensor_tensor(out=v(mixedT)[:, ch, :], in0=v(cur)[:, 16:528],
                                                      in1=v(h)[:, 16:528], op=ALU.subtract),
             reads=[cur, h], writes=[mixedT])

    pooltmp = [(sb("ptA%d" % i, W0 + 16384 + i * WS, [528]), sb("ptB%d" % i, W0 + 16384 + 2112 + i * WS, [528]))
               for i in range(2)]

    def do_tile(ti):
        own = ti >= nctx
        oi = ti - nctx
        s0 = ti * T
        xT = actA
        for sub in range(4):
            xs = xst[sub % 2]
            P.op("sp", lambda s, xs=xs, sub=sub: s.dma_start(out=v(xs), in_=xall[s0 + sub * 128:s0 + (sub + 1) * 128, :]),
                 writes=[xs], dma=xs.name)
            transpose_rows(xs, v(xs), xT, sub * 128)
        P.op("sp", lambda s: s.dma_start(out=v(cosb), in_=cos_d[:, s0:s0 + T]), writes=[cosb], dma="cos")
        P.op("sp", lambda s: s.dma_start(out=v(sinb), in_=sin_d[:, s0:s0 + T]), writes=[sinb], dma="sin")
        mark(31)
        if (own or ti == nctx - 1) and os.environ.get("KNOPOOL", "0") == "0":
            if own:
                P.op("sp", lambda s: s.dma_start(out=v(invc), in_=invc_d[:, :, oi * T:(oi + 1) * T]),
                     writes=[invc], dma="invc")
            proj_fm("in", xT, KC, 0, 16, lambda n, pb: pool_chunk(n, pb, not own))
        mark(32)
        if own:
            proj_fm("in", xT, KC, 2048, HEADS, lambda n, pb: rope_consume(pb, v(qT)[:, n, :], qT))
        proj_fm("in", xT, KC, int(os.environ.get("KC0", "4096")), HEADS,
                lambda n, pb: rope_consume(pb, v(kst)[:, n, :], kst, ksum_h=n, blk0=2 * ti))
        mark(33)
        P.op("sp", lambda s: s.dma_start(out=kt_scr[:, :, s0:s0 + T].rearrange("h d t -> d h t"), in_=v(kst)),
             reads=[kst], writes=[dr("kt_%d" % ti)], dma="kst")
        mark(34)
        loaded = {}

        def ensv(i):
            if i < 8 and i not in loaded:
                loaded[i] = load_panel("in", 0, KC, 6144 + i * 256, 256)
        ensv(0)
        for p in range(8):
            ensv(p + 1)
            slot = loaded.pop(p)
            pbs = [next_ps(), next_ps()]

            def f(t, slot=slot, pbs=pbs):
                ins = None
                for sub in range(4):
                    out = pbs[sub // 2].ap[:, (sub % 2) * 256:(sub % 2) * 256 + 256]
                    for kc in range(KC):
                        ins = t.matmul(out, lhsT=xT.ap[:, kc, sub * 128:(sub + 1) * 128], rhs=slot.ap[:, kc, 0:256],
                                       start=(kc == 0), stop=(kc == KC - 1))
                return ins
            P.op("pe", f, reads=[slot, xT], writes=pbs)
            for q in range(2):
                copy_op(ev_eng(), v(vst)[:, 2 * q:2 * q + 2, p * 256:(p + 1) * 256],
                        pbs[q].ap.rearrange("p (a b) -> p a b", a=2), [pbs[q]], [vst])
        P.op("sp", lambda s: s.dma_start(out=v_scr[s0:s0 + T, :].rearrange("(s p) n -> p s n", p=128), in_=v(vst)),
             reads=[vst], writes=[dr("v_%d" % ti)], dma="vst")
        P.op("dve", lambda e: e.tensor_scalar(out=v(kmT)[:, :, 2 * ti:2 * ti + 2], in0=v(kmsum)[:, :, 2 * ti:2 * ti + 2],
                                              scalar1=1.0 / 256.0, scalar2=None, op0=ALU.mult),
             reads=[kmsum], writes=[kmT])
        if not own:
            mark(3)
            return
        mark(4)

        mixT = actA
        for g in range(4):
            slot = wslot_s[wrr[0] % 2]
            wrr[0] += 1
            P.op("sp", lambda s, slot=slot, g=g: s.dma_start(
                out=slot.ap[:, 0:8, :].rearrange("p (a b) n -> p a (b n)", a=4),
                in_=wb["pool"][g * 512:(g + 1) * 512, :].rearrange("(k p) n -> p k n", p=128)),
                reads=[dr("wb_pool")], writes=[slot], dma="w_" + slot.name)
            wv = slot.ap[:, 0:8, :].rearrange("p (a b) n -> p a (b n)", a=4)
            for nn in range(4):
                pb = next_ps()

                def f(t, wv=wv, nn=nn, pb=pb, g=g):
                    ins = None
                    for kc in range(4):
                        ins = t.matmul(pb.ap, lhsT=wv[:, kc, nn * 128:(nn + 1) * 128], rhs=v(mixedT)[:, 4 * g + kc, :],
                                       start=(kc == 0), stop=(kc == 3))
                    return ins
                P.op("pe", f, reads=[slot, mixedT], writes=[pb])
                n = 4 * g + nn
                P.op("dve", lambda e, pb=pb, n=n: e.tensor_scalar(out=v(mixT)[:, n, :], in0=pb.ap,
                                                                scalar1=v(pscale)[:, n:n + 1], scalar2=None, op0=ALU.mult),
                     reads=[pb, pscale], writes=[mixT])

        mark(5)
        nkc = (ti + 1) * 4
        kvslots = [(sb("kts0", W0, [NSLOT], BF16), sb("vs0", W0 + 16384, [NSLOT // 128, 128], BF16)),
                   (sb("kts1", B0, [NSLOT], BF16), sb("vs1", B0 + 16384, [NSLOT // 128, 128], BF16))]
        pT = [sb("pT%d" % i, B0 + 49152 + i * 1024, [T], BF16) for i in range(3)]
        maskT = [sb("maskT%d" % i, B0 + 52224 + i * 1024, [T], BF16, parts=32) for i in range(2)]
        rinv = sb("rinv", B0 + 54272, [T])
        scale = 128.0 ** -0.5
        mb8 = [[sb("mb8_%d_%d" % (i, j), E0 + 5120 + (i * 4 + j) * 64, [NBLK], BF16) for j in range(4)] for i in range(2)]
        racc = [sb("racc%d" % i, E0 + 5632 + i * 2048, [T]) for i in range(2)]
        rhi = sb("rhi", E0 + 9728, [T], BF16)
        rlo = sb("rlo", E0 + 10752, [T], BF16)

        def load_kv(h):
            kts, vs = kvslots[h % 2]
            P.op("sp", lambda s: s.dma_start(out=v(kts)[:, 0:nkc * 128], in_=kt_scr[h, :, 0:nkc * 128]),
                 reads=[dr("kt_%d" % i) for i in range(ti + 1)], writes=[kts], dma="kv_" + kts.name)
            P.op("sp", lambda s: s.dma_start(
                out=v(vs)[:, 0:nkc, :],
                in_=v_scr[0:nkc * 128, h * 128:(h + 1) * 128].rearrange("(c p) d -> p c d", p=128)),
                reads=[dr("v_%d" % i) for i in range(ti + 1)], writes=[vs], dma="kv_" + vs.name)

        def mask_a(h):
            pg = PB[0]
            for sub in range(4):
                ob = 2 * oi + sub // 2
                mb = mb8[h % 2][sub]
                P.op("pe", lambda t, sub=sub: t.matmul(
                    pg.ap[:, sub * NBLK:(sub + 1) * NBLK], lhsT=v(qT)[:, h, sub * 128:(sub + 1) * 128], rhs=v(kmT)[:, h, :],
                    start=True, stop=True), reads=[qT, kmT], writes=[pg])
                P.op("dve", lambda e, sub=sub, ob=ob: e.tensor_tensor(out=v(g2), in0=pg.ap[:, sub * NBLK:(sub + 1) * NBLK],
                                                                     in1=v(vbg)[:, ob, :], op=ALU.add),
                     reads=[pg, vbg], writes=[g2])
                P.op("dve", lambda e: e.max(out=v(top8), in_=v(g2)), reads=[g2], writes=[top8])
                P.op("dve", lambda e: e.tensor_scalar(out=v(selb), in0=v(g2), scalar1=v(top8)[:, 2:3], scalar2=BIG,
                                                      op0=ALU.is_ge, op1=ALU.mult),
                     reads=[g2, top8], writes=[selb])
                P.op("dve", lambda e, ob=ob: e.tensor_tensor(out=v(selb), in0=v(selb), in1=v(vbgm)[:, ob, :], op=ALU.add),
                     reads=[selb, vbgm], writes=[selb])
                P.op("dve", lambda e, ob=ob, mb=mb: e.tensor_tensor(out=v(mb), in0=v(selb), in1=v(ndg)[:, ob, :],
                                                                   op=ALU.mult),
                     reads=[selb, ndg], writes=[mb])

        def mask_b(h):
            pg = PB[1]
            mT = maskT[h % 2]

            def f(t):
                ins = None
                for sub in range(4):
                    ins = t.transpose(out=pg.ap.bitcast(BF16)[0:NBLK, sub * 128:(sub + 1) * 128], in_=v(mb8[h % 2][sub]),
                                      identity=v(identb))
                return ins
            P.op("pe", f, reads=mb8[h % 2] + [identb], writes=[pg])
            P.op("act", lambda e: e.copy(out=v(mT), in_=pg.ap.bitcast(BF16)[0:NBLK, 0:T]), reads=[pg], writes=[mT])

        load_kv(0)
        mask_a(0)
        mask_b(0)
        if HEADS > 1:
            load_kv(1)
        pos_a = 1
        pos_b = max(2, (3 * nkc) // 4)
        for h in range(HEADS):
            kts, vs = kvslots[h % 2]
            mT = maskT[h % 2]
            psO, psR = (PB[5], PB[6]) if h % 2 == 0 else (PB[7], PB[2])
            sbanks = (3, 4)
            for step in range(nkc + 1):
                ci = step
                if ci < nkc:
                    if h + 1 < HEADS and ci == pos_a:
                        mask_a(h + 1)
                    if h + 1 < HEADS and ci == pos_b:
                        mask_b(h + 1)
                    j = ci // 2
                    pS = PB[sbanks[ci % 2]]
                    pt = pT[ci % 3]

                    def fs(t, pS=pS, ci=ci, j=j, kts=kts, mT=mT, h=h):
                        t.matmul(pS.ap, lhsT=v(kts)[:, ci * 128:(ci + 1) * 128], rhs=v(qT)[:, h, :], start=True, stop=False)
                        return t.matmul(pS.ap, lhsT=v(eblk)[:, j, :], rhs=v(mT), start=False, stop=True)
                    P.op("pe", fs, reads=[kts, qT, eblk, mT], writes=[pS])
                    P.op("act", lambda e, pS=pS, pt=pt: e.activation(out=v(pt), in_=pS.ap, func=AF.Exp, scale=scale),
                         reads=[pS], writes=[pt])
                    cl = ci - ti * 4
                    if cl >= 0:
                        P.op("pool", lambda e, pt=pt, cl=cl: e.tensor_tensor(out=v(pt), in0=v(pt), in1=v(cm)[:, cl, :],
                                                                            op=ALU.mult),
                             reads=[pt, cm], writes=[pt])
                if step >= 1:
                    cj = step - 1
                    ptj = pT[cj % 3]

                    def fo(t, pt=ptj, ci=cj, vs=vs, psO=psO):
                        return t.matmul(psO.ap, lhsT=v(vs)[:, ci, :], rhs=v(pt), start=(ci == 0), stop=(ci == nkc - 1))
                    P.op("pe", fo, reads=[vs, ptj], writes=[psO])
                    ra = racc[h % 2]
                    if cj == 0:
                        P.op("dve", lambda e, ra=ra, pt=ptj: e.tensor_copy(out=v(ra), in_=v(pt)), reads=[ptj], writes=[ra])
                    else:
                        P.op("dve", lambda e, ra=ra, pt=ptj: e.tensor_tensor(out=v(ra), in0=v(ra), in1=v(pt), op=ALU.add),
                             reads=[ra, ptj], writes=[ra])
            if h + 2 < HEADS:
                load_kv(h + 2)
            ra = racc[h % 2]
            P.op("dve", lambda e, ra=ra: e.tensor_copy(out=v(rhi), in_=v(ra)), reads=[ra], writes=[rhi])
            P.op("dve", lambda e, ra=ra: e.tensor_tensor(out=v(ra), in0=v(ra), in1=v(rhi), op=ALU.subtract),
                 reads=[ra, rhi], writes=[ra])
            P.op("dve", lambda e, ra=ra: e.tensor_copy(out=v(rlo), in_=v(ra)), reads=[ra], writes=[rlo])

            def fr(t, psR=psR):
                t.matmul(psR.ap, lhsT=v(ones), rhs=v(rhi), start=True, stop=False)
                return t.matmul(psR.ap, lhsT=v(ones), rhs=v(rlo), start=False, stop=True)
            P.op("pe", fr, reads=[ones, rhi, rlo], writes=[psR])
            P.op("dve", lambda e, psR=psR: e.reciprocal(out=v(rinv), in_=psR.ap), reads=[psR], writes=[rinv])
            P.op("dve", lambda e, psO=psO, h=h: e.tensor_tensor(out=v(mixT)[:, 16 + h, :], in0=psO.ap, in1=v(rinv),
                                                               op=ALU.mult),
                 reads=[psO, rinv], writes=[mixT])

        mark(6)
        ystB = [sb("ystB%d_" % i, B0 + i * 4096, [4, 256]) for i in range(2)]
        ystA = [sb("ystA%d_" % i, A0 + i * 4096, [4, 256]) for i in range(2)]
        proj_tm("out", mixT, KC, ystB, dst=y_scr, dst_res=dr("y_scr"))
        lnb = ln_bufs()
        hT = actA
        ln_stage(dr("y_scr"), xall[s0:s0 + T, :], dr("xin"), 0, h_scr, dr("h_scr"), hT, lnb)
        mark(7)
        q2T = sb("q2T", B0, [KC, T], BF16)
        sc2 = 1024.0 ** -0.5
        proj_fm("xq", hT, KC, 0, KC, lambda n, pb: copy_op(ev_eng(), v(q2T)[:, n, :], pb.ap, [pb], [q2T]))
        kmTs = sb("kmTs", B0 + 32768, [KC, MEM], BF16)
        vms = sb("vms", B0 + 49152, [2, D], BF16)
        pT2 = [sb("pT2_%d" % i, B0 + 65536 + i * 2048, [2, T], BF16) for i in range(2)]
        rinv2 = sb("rinv2", B0 + 69632, [T])
        P.op("sp", lambda s: s.dma_start(out=v(kmTs), in_=kmT_scr.rearrange("(k p) m -> p k m", p=128)),
             reads=[dr("kmT_scr")], writes=[kmTs], dma="kmTs")
        P.op("sp", lambda s: s.dma_start(out=v(vms), in_=vm_scr.rearrange("(s p) n -> p s n", p=128)),
             reads=[dr("vm_scr")], writes=[vms], dma="vms")
        o2T = actA
        for xh in range(4):
            p2 = pT2[xh % 2]
            for mc in range(2):
                pS = next_ps((3, 4))

                def fs(t, pS=pS, mc=mc, xh=xh):
                    ins = None
                    for kc in range(8):
                        ins = t.matmul(pS.ap, lhsT=v(kmTs)[:, xh * 8 + kc, mc * 128:(mc + 1) * 128],
                                       rhs=v(q2T)[:, xh * 8 + kc, :], start=(kc == 0), stop=(kc == 7))
                    return ins
                P.op("pe", fs, reads=[kmTs, q2T], writes=[pS])
                P.op("act", lambda e, pS=pS, p2=p2, mc=mc: e.activation(out=v(p2)[:, mc, :], in_=pS.ap, func=AF.Exp,
                                                                       scale=sc2),
                     reads=[pS], writes=[p2])
            psR = PB[2]

            def fr(t, p2=p2, psR=psR):
                t.matmul(psR.ap, lhsT=v(ones), rhs=v(p2)[:, 0, :], start=True, stop=False)
                return t.matmul(psR.ap, lhsT=v(ones), rhs=v(p2)[:, 1, :], start=False, stop=True)
            P.op("pe", fr, reads=[ones, p2], writes=[psR])
            P.op("dve", lambda e, psR=psR: e.reciprocal(out=v(rinv2), in_=psR.ap), reads=[psR], writes=[rinv2])
            for dc in range(8):
                psO = next_ps((5, 6, 7))

                def fo(t, p2=p2, psO=psO, dc=dc, xh=xh):
                    c0 = xh * 1024 + dc * 128
                    t.matmul(psO.ap, lhsT=v(vms)[:, 0, c0:c0 + 128], rhs=v(p2)[:, 0, :], start=True, stop=False)
                    return t.matmul(psO.ap, lhsT=v(vms)[:, 1, c0:c0 + 128], rhs=v(p2)[:, 1, :], start=False, stop=True)
                P.op("pe", fo, reads=[vms, p2], writes=[psO])
                P.op("dve", lambda e, psO=psO, dc=dc, xh=xh: e.tensor_tensor(out=v(o2T)[:, xh * 8 + dc, :], in0=psO.ap,
                                                                            in1=v(rinv2), op=ALU.mult),
                     reads=[psO, rinv2], writes=[o2T])
        mark(8)
        proj_tm("xo", o2T, KC, ystB, dst=y_scr, dst_res=dr("y_scr"))
        h2T = actA
        ln_stage(dr("y_scr"), h_scr, dr("h_scr"), 2, h_scr, dr("h_scr"), h2T, lnb)
        mark(9)
        actT = sb("actT", B0, [FC, T], BF16)
        sg = [sb("sg%d" % i, W0 + 32768 + i * 2048, [T]) for i in range(2)]

        def load_f(name, n, slot):
            src = wb[name][:, n * 128:(n + 1) * 128].rearrange("(k p) n -> p k n", p=128)
            P.op("sp", lambda s: s.dma_start(out=slot.ap, in_=src), reads=[dr("wb_" + name)], writes=[slot],
                 dma="w_" + slot.name)
            return slot
        fl = {}

        def ensf(n):
            if n < FC and n not in fl:
                fl[n] = (load_f("gate", n, fslot[(n % 2) * 2]), load_f("up", n, fslot[(n % 2) * 2 + 1]))
        ensf(0)
        for n in range(FC):
            ensf(n + 1)
            sl_g, sl_u = fl.pop(n)
            pg, pu = next_ps(), next_ps()

            def fg(t, sl=sl_g, pb=pg):
                ins = None
                for kc in range(KC):
                    ins = t.matmul(pb.ap, lhsT=sl.ap[:, kc, 0:128], rhs=v(h2T)[:, kc, :], start=(kc == 0), stop=(kc == KC - 1))
                return ins
            P.op("pe", fg, reads=[sl_g, h2T], writes=[pg])

            def fu(t, sl=sl_u, pb=pu):
                ins = None
                for kc in range(KC):
                    ins = t.matmul(pb.ap, lhsT=sl.ap[:, kc, 0:128], rhs=v(h2T)[:, kc, :], start=(kc == 0), stop=(kc == KC - 1))
                return ins
            P.op("pe", fu, reads=[sl_u, h2T], writes=[pu])
            s_ = sg[n % 2]
            P.op("act", lambda e, s_=s_, pg=pg: e.activation(out=v(s_), in_=pg.ap, func=AF.Silu), reads=[pg], writes=[s_])
            P.op("dve", lambda e, s_=s_, pu=pu, n=n: e.tensor_tensor(out=v(actT)[:, n, :], in0=pu.ap, in1=v(s_), op=ALU.mult),
                 reads=[pu, s_], writes=[actT])
        mark(10)
        proj_tm("down", actT, FC, ystA, dst=y_scr, dst_res=dr("y_scr"))
        lnb = ln_bufs()
        ln_stage(dr("y_scr"), h_scr, dr("h_scr"), 4, out_d[oi * T:(oi + 1) * T, :], dr("out"), None, lnb)

    for ti_ in range(NT):
        if ti_ == nctx:
            mem_stage()
        do_tile(ti_)

    P.frozen = False
    P.final_op = P.op("sp", None, reads=[dr("out")])

    with nc.Block() as block:
        P.emit(nc, block, stack)
    stack.close()
    return nc


def _host_tables(qt, nctx=12, nown=4):
    NSLOT = (nctx + nown) * T
    nctx_tok = nctx * T
    nvalid_tok = min(qt * nown * T, nctx_tok)
    pos = np.zeros(NSLOT, np.int64)
    pos[nctx_tok - nvalid_tok:nctx_tok] = np.arange(nvalid_tok) + (qt * nown * T - nvalid_tok)
    pos[nctx_tok:] = qt * nown * T + np.arange(nown * T)
    half = 64
    inv = (np.float32(10000.0) ** (-np.arange(half, dtype=np.float32) / np.float32(half))).astype(np.float32)
    ang = pos.astype(np.float32)[:, None] * inv[None, :]
    cos = np.cos(ang).astype(np.float32).T
    sin = np.sin(ang).astype(np.float32).T
    cosT = np.ascontiguousarray(np.concatenate([cos, cos], 0))
    sinT = np.ascontiguousarray(np.concatenate([sin, sin], 0))
    gpos = qt * nown * T + np.arange(nown * T)
    invc = np.stack([1.0 / np.minimum(gpos + 1, w) for w in WINS]).astype(np.float32)
    invc = np.ascontiguousarray(np.broadcast_to(invc[None], (128, 4, nown * T)))
    nvalid_blk = nvalid_tok // 256
    vbg = np.full((8, NBLK), -BIG, np.float32)
    ndg = np.ones((8, NBLK), np.float32)
    for ob in range(2 * nown):
        vbg[ob, 2 * nctx - nvalid_blk:2 * nctx] = 0.0
        vbg[ob, 2 * nctx:2 * nctx + ob] = 0.0
        ndg[ob, 2 * nctx + ob] = 0.0
    vbg = np.ascontiguousarray(np.broadcast_to(vbg[None], (128, 8, NBLK)))
    ndg = np.ascontiguousarray(np.broadcast_to(ndg[None], (128, 8, NBLK)))
    return cosT, sinT, invc, vbg, ndg


def _consts():
    k = np.arange(128)[:, None]
    cmk = np.zeros((128, 4, T), np.float32)
    for cl in range(4):
        cmk[:, cl, :] = ((cl * 128 + k) <= np.arange(T)[None, :]).astype(np.float32)
    eb = np.zeros((32, NBLK, 128), np.float32)
    for j in range(NBLK):
        eb[j, j, :] = 1.0
    psw = np.zeros((128, 128), np.float32)
    for m in range(64):
        psw[m + 64, m] = -1.0
        psw[m, m + 64] = 1.0
    return cmk, eb, psw, np.eye(128, dtype=np.float32)


_NC_CACHE = {}


def kernel(x, mem, w_mix_in, w_pool, pool_scale, w_mix_out, ln1_g, ln1_b, w_xq, w_xkv, w_xo,
           ln2_g, ln2_b, w_gate, w_up, w_down, ln3_g, ln3_b):
    x = np.asarray(x, np.float32)
    mem = np.asarray(mem, np.float32)
    B, S, _ = x.shape
    nq = 4
    own = S // nq
    if "nc" not in _NC_CACHE:
        _NC_CACHE["nc"] = build()
    nc = _NC_CACHE["nc"]
    cmk, eb, psw, idn = _consts()
    lnp = np.ascontiguousarray(np.stack([np.asarray(a, np.float32)[0] for a in (ln1_g, ln1_b, ln2_g, ln2_b, ln3_g, ln3_b)]))
    psc = np.ascontiguousarray(np.asarray(pool_scale, np.float32)[0].reshape(16, 128).T)
    shared = {
        "w_in": np.asarray(w_mix_in, np.float32)[0], "w_pool": np.asarray(w_pool, np.float32)[0].reshape(2048, 512),
        "w_out": np.asarray(w_mix_out, np.float32)[0], "w_xq": np.asarray(w_xq, np.float32)[0],
        "w_xkv": np.asarray(w_xkv, np.float32)[0], "w_xo": np.asarray(w_xo, np.float32)[0],
        "w_gate": np.asarray(w_gate, np.float32)[0], "w_up": np.asarray(w_up, np.float32)[0],
        "w_down": np.asarray(w_down, np.float32)[0], "lnp": lnp, "pscale": psc,
        "cm": cmk, "eblk": eb, "pswap": psw, "ident": idn,
    }
    in_maps = []
    for c in range(8):
        b, qt = c // nq, c % nq
        xall = np.zeros((3 * own + own, D), np.float32)
        xall[3 * own - qt * own:3 * own] = x[b, :qt * own]
        xall[3 * own:] = x[b, qt * own:(qt + 1) * own]
        cosT, sinT, invc, vbg, ndg = _host_tables(qt)
        m = dict(shared)
        m.update({"xall": xall, "mem": mem[b], "cosT": cosT, "sinT": sinT, "invc": invc, "vbg": vbg, "ndg": ndg})
        in_maps.append(m)
    res = run_bass_kernel_spmd(nc, in_maps, core_ids=list(range(8)))
    out = np.empty((B, S, D), np.float32)
    for c in range(8):
        b, qt = c // nq, c % nq
        out[b, qt * own:(qt + 1) * own] = np.asarray(res.results[c]["out"], np.float32)
    return out
```

```python
import os
import numpy as np
import concourse.bass as bass
import concourse.mybir as mybir
from concourse.bass_utils import run_bass_kernel_spmd

F32, BF16 = mybir.dt.float32, mybir.dt.bfloat16
AF = mybir.ActivationFunctionType
ALU = mybir.AluOpType
AX = mybir.AxisListType

D = 4096
KC = 32
T = 512
HEADS = 16
FFN = 11008
FC = 86
MEM = 256
BIG = 30000.0
ALPHA = 2.0 ** 0.25
EPS = 1e-5
NBLK = 32
WINS = (2, 4, 8, 16)


class Buf:
    __slots__ = ("name", "space", "lo", "hi", "ap", "overl", "last_w", "rd", "rd_dma")

    def __init__(self, name, space, lo, hi, ap):
        self.name, self.space, self.lo, self.hi, self.ap = name, space, lo, hi, ap
        self.overl = [self]
        self.last_w = None
        self.rd = {}
        self.rd_dma = []


class Op:
    __slots__ = ("eng", "fn", "deps", "dma", "cnt", "need_inc", "idx")


def _semname(n):
    if n.startswith("c_"):
        return "const"
    if n == "cast_in_kv":
        return "castA"
    if n in ("cast_in_pq", "cast_pool"):
        return "castB"
    if n == "cast_xkv":
        return "castC"
    if n.startswith("cast_"):
        return "castD"
    if n in ("cos", "sin"):
        return "cs"
    if n.startswith("kv_"):
        return "kv" + n[-1]
    if n.startswith("w_wslot"):
        return "w" + n[7]
    if n.startswith("w_fslot"):
        return "f" + str(int(n[7]) // 2)
    if n.startswith("yst_"):
        return "yst" + n.rstrip("_")[-1]
    if n in ("ln_g", "ln_b"):
        return "lngb"
    if n in ("kmTs", "vms"):
        return "xkv"
    if n.startswith("st_"):
        return "st" + n[-1]
    return n


class Prog:
    ENGS = ("pe", "act", "dve", "pool", "sp")

    def __init__(self):
        self.ops = {e: [] for e in self.ENGS}
        self.bufs = {}
        self.dma_cnt = {}
        self.nops = 0

    def buf(self, name, space, lo, nbytes, ap):
        b = Buf(name, space, lo, lo + nbytes, ap)
        lst = self.bufs.setdefault(space, [])
        for o in lst:
            if o.lo < b.hi and b.lo < o.hi:
                o.overl.append(b)
                b.overl.append(o)
        lst.append(b)
        return b

    frozen = False

    def op(self, eng, fn, reads=(), writes=(), dma=None):
        if self.frozen:
            return None
        if dma is not None:
            dma = _semname(dma)
        o = Op()
        o.eng, o.fn, o.dma, o.need_inc, o.cnt = eng, fn, dma, False, 0
        o.idx = self.nops
        self.nops += 1
        deps = {}

        def add(d):
            if d is None:
                return
            if d.dma is None and d.eng == "pe" and eng == "pe" and dma is None:
                return
            if d.dma is not None:
                deps[id(d)] = (d, self.dma_cnt[d.dma])
            else:
                deps[id(d)] = (d, None)

        for b in reads:
            for ob in b.overl:
                add(ob.last_w)
                if b.space == "ps":
                    for r in ob.rd.values():
                        if r.eng != eng:
                            add(r)
        for b in writes:
            for ob in b.overl:
                add(ob.last_w)
                for r in ob.rd.values():
                    add(r)
                for r in ob.rd_dma:
                    add(r)
        o.deps = list(deps.values())
        for d, _ in o.deps:
            d.need_inc = True
        for b in writes:
            b.last_w = o
            b.rd = {}
            b.rd_dma = []
        for b in reads:
            if dma is not None:
                b.rd_dma.append(o)
            else:
                b.rd[eng] = o
        if dma is not None:
            self.dma_cnt[dma] = self.dma_cnt.get(dma, 0) + 16
            o.cnt = self.dma_cnt[dma]
        self.ops[eng].append(o)
        return o

    def emit(self, nc, block, stack):
        sems = {}

        def sem(name):
            if name not in sems:
                sems[name] = stack.enter_context(nc.semaphore("s_" + name))
            return sems[name]

        cnt = {e: 0 for e in self.ENGS}
        allops = sorted((o for e in self.ENGS for o in self.ops[e]), key=lambda o: o.idx)
        for o in allops:
            if o.dma is None and o.need_inc:
                cnt[o.eng] += 1
                o.cnt = cnt[o.eng]
        for e in self.ENGS:
            sem("e_" + e)
        for name in self.dma_cnt:
            sem("d_" + name)

        final_op = self.final_op

        def run(eng_name):
            def body(e):
                waited = {}
                for o in self.ops[eng_name]:
                    need = {}
                    for d, v in o.deps:
                        if d.dma is not None:
                            k, val = "d_" + d.dma, v
                        else:
                            k, val = "e_" + d.eng, d.cnt
                        if val > need.get(k, 0):
                            need[k] = val
                    if o.dma is not None and o.cnt > 16:
                        k = "d_" + o.dma
                        need[k] = max(need.get(k, 0), o.cnt - 16)
                    wl = []
                    for k, val in need.items():
                        if waited.get(k, 0) < val:
                            e.wait_ge(sems[k], val)
                            waited[k] = val
                            wl.append((k, val))
                    if os.environ.get("KDUMP"):
                        print("OP", eng_name, o.idx, "waits", wl, "inc", (o.dma, o.cnt) if o.dma else (o.cnt if o.need_inc else None),
                              getattr(o, "tag", ""))
                    if o.fn is None:
                        if o is final_op:
                            for nm, tot in self.dma_cnt.items():
                                e.wait_ge(sems["d_" + nm], tot)
                        continue
                    ins = o.fn(e)
                    if o.dma is not None:
                        ins.then_inc(sems["d_" + o.dma], 16)
                    elif o.need_inc:
                        ins.then_inc(sems["e_" + eng_name], 1)
            return body

        block.tensor(run("pe"))
        block.scalar(run("act"))
        block.vector(run("dve"))
        block.gpsimd(run("pool"))
        block.sync(run("sp"))


def build(nctx=12, nown=4, debug=False):
    nc = bass.Bass("TRN2", target_bir_lowering=False)
    NT = nctx + nown
    NSLOT = NT * T
    CTXB = nctx * 2

    def din(name, shape, dt=F32):
        return nc.dram_tensor(name, shape, dt, kind="ExternalInput").ap()

    def dscr(name, shape, dt):
        return nc.dram_tensor(name, shape, dt, kind="Internal").ap()

    xall = din("xall", [NSLOT, D])
    mem = din("mem", [MEM, D])
    w_in = din("w_in", [D, 2 * D])
    w_pool = din("w_pool", [4 * 512, 512])
    w_out = din("w_out", [D, D])
    w_xq = din("w_xq", [D, D])
    w_xkv = din("w_xkv", [D, 2 * D])
    w_xo = din("w_xo", [D, D])
    w_gate = din("w_gate", [D, FFN])
    w_up = din("w_up", [D, FFN])
    w_down = din("w_down", [FFN, D])
    lnp = din("lnp", [6, D])
    pscale_d = din("pscale", [128, 16])
    cos_d = din("cosT", [128, NSLOT])
    sin_d = din("sinT", [128, NSLOT])
    invc_d = din("invc", [128, 4, nown * T])
    vbg_d = din("vbg", [128, 8, NBLK])
    ndg_d = din("ndg", [128, 8, NBLK])
    cm_d = din("cm", [128, 4, T])
    eb_d = din("eblk", [32, NBLK, 128])
    psw_d = din("pswap", [128, 128])
    idn_d = din("ident", [128, 128])
    out_d = nc.dram_tensor("out", [nown * T, D], F32, kind="ExternalOutput").ap()

    wb = {
        "in": dscr("wb_in", [D, 2 * D], BF16), "pool": dscr("wb_pool", [2048, 512], BF16),
        "out": dscr("wb_out", [D, D], BF16), "xq": dscr("wb_xq", [D, D], BF16),
        "xkv": dscr("wb_xkv", [D, 2 * D], BF16), "xo": dscr("wb_xo", [D, D], BF16),
        "gate": dscr("wb_gate", [D, FFN], BF16), "up": dscr("wb_up", [D, FFN], BF16),
        "down": dscr("wb_down", [FFN, D], BF16),
    }
    wsrc = {"in": w_in, "pool": w_pool, "out": w_out, "xq": w_xq, "xkv": w_xkv, "xo": w_xo,
            "gate": w_gate, "up": w_up, "down": w_down}
    kt_scr = dscr("kt_scr", [HEADS, 128, NSLOT], BF16)
    v_scr = dscr("v_scr", [NSLOT, 2048], BF16)
    kmT_scr = dscr("kmT_scr", [D, MEM], BF16)
    vm_scr = dscr("vm_scr", [MEM, D], BF16)
    y_scr = dscr("y_scr", [T, D], F32)
    h_scr = dscr("h_scr", [T, D], F32)

    P = Prog()
    import os
    STOP = int(os.environ.get("KSTOP", "99"))

    def mark(n):
        if STOP == n:
            P.frozen = True
    import contextlib
    stack = contextlib.ExitStack()
    ARENA = 204800
    E0 = 190464
    arena = stack.enter_context(nc.sbuf_tensor("arena", [128, ARENA // 4], F32))
    psum = stack.enter_context(nc.psum_tensor("psum", [128, 8, 512], F32))

    def sb(name, off, shape, dt=F32, parts=128):
        esz = 4 if dt == F32 else 2
        n = int(np.prod(shape))
        nbytes = n * esz
        assert off % 4 == 0 and off + nbytes <= ARENA, (name, off, nbytes)
        ap = arena[0:parts, off // 4:(off + nbytes + 3) // 4]
        if dt != F32:
            ap = ap.bitcast(dt)
        if len(shape) == 2:
            ap = ap.rearrange("p (a b) -> p a b", a=shape[0])
        elif len(shape) == 3:
            ap = ap.rearrange("p (a b c) -> p a b c", a=shape[0], b=shape[1])
        return P.buf(name, "sb", off, nbytes, ap)

    PB = [P.buf("ps%d" % i, "ps", i * 2048, 2048, psum[:, i, :]) for i in range(8)]
    dres = {}

    def dr(name):
        if name not in dres:
            dres[name] = P.buf(name, "dram:" + name, 0, 1, None)
        return dres[name]

    W0, W1, A0, B0, C0 = 0, 22528, 45056, 77824, 165888
    WS = 22528
    c = C0
    identf = sb("identf", c, [128]); c += 512
    identb = sb("identb", c, [128], BF16); c += 256
    pswap = sb("pswap", c, [128], BF16); c += 256
    ones = sb("ones", c, [128], BF16); c += 256
    eblk = sb("eblk", c, [NBLK, 128], BF16, parts=32); c += 8192
    cm = sb("cm", c, [4, T], BF16); c += 4096
    kmsum = sb("kmsum", c, [HEADS, NBLK]); c += 2048
    kmT = sb("kmT", c, [HEADS, NBLK], BF16); c += 1024
    vbg = sb("vbg", c, [8, NBLK]); c += 1024
    vbgm = sb("vbgm", c, [8, NBLK]); c += 1024
    ndg = sb("ndg", c, [8, NBLK]); c += 1024
    halo = sb("halo", c, [16, 16]); c += 1024
    pscale = sb("pscale", c, [16]); c += 64
    epsc = sb("epsc", c, [1]); c += 4
    g2 = sb("g2", c, [NBLK]); c += 128
    top8 = sb("top8", c, [8]); c += 32
    selb = sb("selb", c, [NBLK]); c += 128
    mbias = [sb("mbias%d" % i, c + 128 * i, [NBLK], BF16) for i in range(2)]; c += 256
    stats = sb("stats", c, [8, 6]); c += 192
    mv = sb("mv", c, [2]); c += 8
    rstd = sb("rstd", c, [1]); c += 4
    ksum2 = sb("ksum2", c, [2]); c += 8
    assert c <= ARENA, c

    wslot_l = [sb("wslot0", W0, [43, 256], BF16), sb("wslot1", W1, [43, 256], BF16)]
    wslot_s = [sb("wslot0s", W0, [32, 256], BF16), sb("wslot1s", W1, [32, 256], BF16)]
    fslot = [sb("fslot%d" % i, W0 + i * 8192, [32, 128], BF16) for i in range(4)]
    actA = sb("actA", A0, [KC, T], BF16)

    def v(b):
        return b.ap

    def cast_weight(name, c0=None, c1=None, tag=None):
        src, dst = wsrc[name], wb[name]
        rows = src.shape[0]
        if c0 is None:
            c0, c1 = 0, src.shape[1]
        tag = tag or name
        step = max(128, (8 << 20) // ((c1 - c0) * 4) // 128 * 128)
        r = 0
        while r < rows:
            e = min(rows, r + step)
            P.op("pool", (lambda g, r=r, e=e: g.dma_start(out=dst[r:e, c0:c1], in_=src[r:e, c0:c1])),
                 writes=[dr("wb_" + tag)], dma="cast_" + tag)
            r = e


    def load_const(dst, src_ap, shape, parts=128, conv=True, off=0):
        nel = int(np.prod(shape))
        st = sb("cst_%s" % dst.name, B0 + off, shape, F32, parts=parts)
        P.op("sp", lambda s: s.dma_start(out=v(st), in_=src_ap), writes=[st], dma="c_" + dst.name)
        conv_list.append((dst, st))
        return nel * 4

    o = 0
    conv_list = []
    P.op("sp", lambda s: s.dma_start(out=v(identf), in_=idn_d[:, :]), writes=[identf], dma="c_ident")
    o += load_const(pswap, psw_d[:, :], [128], off=o)
    o += load_const(eblk, eb_d[:, :, :], [NBLK, 128], parts=32, off=o)
    o += load_const(cm, cm_d[:, :, :], [4, T], off=o)
    P.op("sp", lambda s: s.dma_start(out=v(vbg), in_=vbg_d[:, :, :]), writes=[vbg], dma="c_vbg")
    P.op("sp", lambda s: s.dma_start(out=v(ndg), in_=ndg_d[:, :, :]), writes=[ndg], dma="c_ndg")
    P.op("sp", lambda s: s.dma_start(out=v(pscale), in_=pscale_d[:, :]), writes=[pscale], dma="c_pscale")
    P.op("dve", lambda e: e.tensor_copy(out=v(identb), in_=v(identf)), reads=[identf], writes=[identb])
    for dst_, st_ in conv_list:
        P.op("dve", lambda e, dst_=dst_, st_=st_: e.tensor_copy(out=v(dst_), in_=v(st_)), reads=[st_], writes=[dst_])
    P.op("dve", lambda e: e.tensor_scalar(out=v(vbgm), in0=v(vbg), scalar1=-BIG, scalar2=None, op0=ALU.add),
         reads=[vbg], writes=[vbgm])
    P.op("dve", lambda e: e.memset(v(ones), 1.0), writes=[ones])
    P.op("dve", lambda e: e.memset(v(epsc), EPS), writes=[epsc])
    P.op("dve", lambda e: e.memset(v(halo), 0.0), writes=[halo])
    P.op("dve", lambda e: e.memset(v(kmsum), 0.0), writes=[kmsum])
    P.op("dve", lambda e: e.memset(v(kmT), 0.0), writes=[kmT])

    cast_weight("in", 4096, 8192, "in_kv")
    cast_weight("in", 0, 4096, "in_pq")
    for name in ("pool", "xkv", "out", "xq", "xo", "gate", "up", "down"):
        cast_weight(name)

    mark(1)
    psrr = [0]

    def next_ps(banks=(2, 3, 4, 5, 6, 7)):
        b = banks[psrr[0] % len(banks)]
        psrr[0] += 1
        return PB[b]

    evrr = [0]

    def ev_eng():
        evrr[0] += 1
        return "act" if evrr[0] % 2 else "dve"

    def copy_op(eng, dst_ap, src_ap, reads, writes):
        if eng == "act":
            P.op("act", lambda e: e.copy(out=dst_ap, in_=src_ap), reads=reads, writes=writes)
        else:
            P.op(eng, lambda e: e.tensor_copy(out=dst_ap, in_=src_ap), reads=reads, writes=writes)

    def transpose_rows(src_buf, src_ap, dstT, col0, ncols=128):
        for k4 in range(KC // 4):
            pb = PB[k4 % 2]

            def f(t, k4=k4, pb=pb):
                ins = None
                for j in range(4):
                    kc = k4 * 4 + j
                    ins = t.transpose(out=pb.ap[:, j * 128:j * 128 + ncols],
                                      in_=src_ap[0:ncols, kc * 128:(kc + 1) * 128],
                                      identity=v(identf)[0:ncols, 0:ncols])
                return ins
            P.op("pe", f, reads=[src_buf, identf], writes=[pb])
            copy_op(ev_eng(), v(dstT)[:, k4 * 4:(k4 + 1) * 4, col0:col0 + ncols],
                    pb.ap.rearrange("p (a b) -> p a b", a=4)[:, :, 0:ncols], [pb], [dstT])

    wrr = [0]

    def load_panel(name, k0, kn, c0, cn):
        slot = (wslot_s if kn <= 32 else wslot_l)[wrr[0] % 2]
        wrr[0] += 1
        src = wb[name][k0 * 128:(k0 + kn) * 128, c0:c0 + cn].rearrange("(k p) n -> p k n", p=128)
        rname = name if name != "in" else ("in_kv" if c0 >= 4096 else "in_pq")
        P.op("sp", lambda s: s.dma_start(out=slot.ap[:, 0:kn, 0:cn], in_=src),
             reads=[dr("wb_" + rname)], writes=[slot], dma="w_" + slot.name)
        return slot

    def proj_fm(name, actT, kn, c0, nchunks, consume, ntok=T, pair=None):
        n = 0
        pend = None
        panels = []
        for p0 in range(0, nchunks, 2):
            cn = min(2, nchunks - p0)
            panels.append((p0, cn))
        loaded = {}

        def ensure(i):
            if i < len(panels) and i not in loaded:
                p0, cn = panels[i]
                loaded[i] = load_panel(name, 0, kn, c0 + p0 * 128, cn * 128)
        ensure(0)
        for i, (p0, cn) in enumerate(panels):
            ensure(i + 1)
            slot = loaded.pop(i)
            for j in range(cn):
                pb = next_ps()

                def f(t, slot=slot, j=j, pb=pb):
                    ins = None
                    for kc in range(kn):
                        ins = t.matmul(pb.ap[:, 0:ntok], lhsT=slot.ap[:, kc, j * 128:(j + 1) * 128],
                                       rhs=actT.ap[:, kc, 0:ntok], start=(kc == 0), stop=(kc == kn - 1))
                    return ins
                P.op("pe", f, reads=[slot, actT], writes=[pb])
                consume(p0 + j, pb)

    def proj_tm(name, actT, kn, ystage, nsub=4, dst=None, dst_res=None, ncols=D, dst_bf16=None):
        halves = [(0, kn)] if kn <= 43 else [(0, 43), (43, kn - 43)]
        npan = ncols // 256
        seq = [(p, h) for p in range(npan) for h in range(len(halves))]
        loaded = {}

        def ensure(i):
            if i < len(seq) and i not in loaded:
                p, h = seq[i]
                loaded[i] = load_panel(name, halves[h][0], halves[h][1], p * 256, 256)
        ensure(0)
        si = 0
        two = len(halves) > 1
        for p in range(npan):
            pbs = [next_ps() for _ in range(nsub if two else (nsub + 1) // 2)]
            for h in range(len(halves)):
                ensure(si + 1)
                slot = loaded.pop(si)
                si += 1
                k0, kk = halves[h]

                def f(t, slot=slot, k0=k0, kk=kk, h=h, pbs=pbs):
                    ins = None
                    for sub in range(nsub):
                        if two:
                            out = pbs[sub].ap[:, 0:256]
                        else:
                            out = pbs[sub // 2].ap[:, (sub % 2) * 256:(sub % 2) * 256 + 256]
                        for kc in range(kk):
                            ins = t.matmul(out, lhsT=actT.ap[:, k0 + kc, sub * 128:(sub + 1) * 128],
                                           rhs=slot.ap[:, kc, 0:256],
                                           start=(h == 0 and kc == 0),
                                           stop=(h == len(halves) - 1 and kc == kk - 1))
                    return ins
                P.op("pe", f, reads=[slot, actT], writes=pbs)
            ys = ystage[p % 2]
            if two:
                for sub in range(nsub):
                    copy_op(ev_eng(), ys.ap[:, sub, :], pbs[sub].ap[:, 0:256], [pbs[sub]], [ys])
            else:
                for q in range((nsub + 1) // 2):
                    ns = min(2, nsub - 2 * q)
                    copy_op(ev_eng(), ys.ap[:, 2 * q:2 * q + ns, :],
                            pbs[q].ap.rearrange("p (a b) -> p a b", a=2)[:, 0:ns, :], [pbs[q]], [ys])
            dd = dst[0:nsub * 128, p * 256:(p + 1) * 256].rearrange("(s p) n -> p s n", p=128)
            P.op("sp", lambda s, ys=ys, dd=dd: s.dma_start(out=dd, in_=ys.ap[:, 0:nsub, :]),
                 reads=[ys], writes=[dst_res], dma="yst_" + ys.name)

    def ln_stage(y_res, resid_ap, resid_res, gi, out_ap, out_res, outT, lnb, nsub=4):
        ybufs, rbufs, gbuf, bbuf = lnb
        P.op("sp", lambda s: s.dma_start(out=v(gbuf), in_=lnp[gi, :].partition_broadcast(128)),
             writes=[gbuf], dma="ln_g")
        P.op("sp", lambda s: s.dma_start(out=v(bbuf), in_=lnp[gi + 1, :].partition_broadcast(128)),
             writes=[bbuf], dma="ln_b")

        def load(sub):
            ybuf, rbuf = ybufs[sub % 2], rbufs[sub % 2]
            rs = slice(sub * 128, (sub + 1) * 128)
            P.op("sp", lambda s: s.dma_start(out=v(ybuf), in_=y_scr[rs, :]),
                 reads=[y_res], writes=[ybuf], dma="ln_y%d" % (sub % 2))
            P.op("sp", lambda s: s.dma_start(out=v(rbuf), in_=resid_ap[rs, :]),
                 reads=[resid_res], writes=[rbuf], dma="ln_r%d" % (sub % 2))

        def stage1(sub):
            ybuf, rbuf = ybufs[sub % 2], rbufs[sub % 2]
            P.op("dve", lambda e: e.scalar_tensor_tensor(out=v(ybuf), in0=v(rbuf), scalar=ALPHA, in1=v(ybuf),
                                                         op0=ALU.mult, op1=ALU.add),
                 reads=[rbuf, ybuf], writes=[ybuf])

            def st(e):
                ins = None
                for i in range(8):
                    ins = e.bn_stats(out=v(stats)[:, i, :], in_=v(ybuf)[:, i * 512:(i + 1) * 512])
                return ins
            P.op("dve", st, reads=[ybuf], writes=[stats])
            P.op("dve", lambda e: e.bn_aggr(out=v(mv), in_=v(stats).rearrange("p a b -> p (a b)")),
                 reads=[stats], writes=[mv])
            P.op("act", lambda e: e.activation(out=v(rstd), in_=v(mv)[:, 1:2], func=AF.Sqrt, bias=v(epsc), scale=1.0),
                 reads=[mv, epsc], writes=[rstd])
            P.op("dve", lambda e: e.reciprocal(out=v(rstd), in_=v(rstd)), reads=[rstd], writes=[rstd])
            P.op("dve", lambda e: e.tensor_scalar(out=v(ybuf), in0=v(ybuf), scalar1=v(mv)[:, 0:1], scalar2=v(rstd),
                                                  op0=ALU.subtract, op1=ALU.mult),
                 reads=[ybuf, mv, rstd], writes=[ybuf])

        def stage2(sub):
            ybuf = ybufs[sub % 2]
            rs = slice(sub * 128, (sub + 1) * 128)
            P.op("pool", lambda e: e.tensor_tensor(out=v(ybuf), in0=v(ybuf), in1=v(gbuf), op=ALU.mult),
                 reads=[ybuf, gbuf], writes=[ybuf])
            P.op("dve", lambda e: e.tensor_tensor(out=v(ybuf), in0=v(ybuf), in1=v(bbuf), op=ALU.add),
                 reads=[ybuf, bbuf], writes=[ybuf])
            P.op("sp", lambda s: s.dma_start(out=out_ap[rs, :], in_=v(ybuf)),
                 reads=[ybuf], writes=[out_res], dma="ln_o")
            if outT is not None:
                transpose_rows(ybuf, v(ybuf), outT, sub * 128)

        load(0)
        if nsub > 1:
            load(1)
        stage1(0)
        for sub in range(nsub):
            if sub + 1 < nsub:
                stage1(sub + 1)
            stage2(sub)
            if sub + 2 < nsub:
                load(sub + 2)

    def ln_bufs():
        return ([sb("ln_y0", B0, [D]), sb("ln_y1", B0 + 32768, [D])],
                [sb("ln_r0", B0 + 16384, [D]), sb("ln_r1", B0 + 49152, [D])],
                sb("ln_g", B0 + 65536, [D]), sb("ln_b", W0, [D]))

    def mem_stage():
        memst = sb("memst", B0, [D])
        memT = sb("memT", A0, [KC, T], BF16)
        for sub in range(2):
            P.op("sp", lambda s, sub=sub: s.dma_start(out=v(memst), in_=mem[sub * 128:(sub + 1) * 128, :]),
                 writes=[memst], dma="xst0")
            transpose_rows(memst, v(memst), memT, sub * 128)
        kmstage = [sb("kmstage%d" % i, B0 + 16384 + i * 512, [MEM], BF16) for i in range(2)]

        def km_consume(n, pb):
            ks = kmstage[n % 2]
            copy_op(ev_eng(), v(ks), pb.ap[:, 0:MEM], [pb], [ks])
            P.op("sp", lambda s, ks=ks, n=n: s.dma_start(out=kmT_scr[n * 128:(n + 1) * 128, :], in_=v(ks)),
                 reads=[ks], writes=[dr("kmT_scr")], dma="st_" + ks.name)
        proj_fm("xkv", memT, KC, 0, KC, km_consume, ntok=MEM)
        ystage_b = [sb("ystB%d" % i, B0 + 20480 + i * 4096, [4, 256]) for i in range(2)]
        vmst = [sb("vmst%d" % i, B0 + 32768 + i * 1024, [2, 256], BF16) for i in range(2)]
        npan = D // 256
        loaded = {}

        def ens(i):
            if i < npan and i not in loaded:
                loaded[i] = load_panel("xkv", 0, KC, D + i * 256, 256)
        ens(0)
        for p in range(npan):
            ens(p + 1)
            slot = loaded.pop(p)
            pb = next_ps()

            def f(t, slot=slot, pb=pb):
                ins = None
                for sub in range(2):
                    for kc in range(KC):
                        ins = t.matmul(pb.ap[:, sub * 256:(sub + 1) * 256], lhsT=memT.ap[:, kc, sub * 128:(sub + 1) * 128],
                                       rhs=slot.ap[:, kc, 0:256], start=(kc == 0), stop=(kc == KC - 1))
                return ins
            P.op("pe", f, reads=[slot, memT], writes=[pb])
            vs = vmst[p % 2]
            copy_op(ev_eng(), v(vs), pb.ap.rearrange("p (a b) -> p a b", a=2), [pb], [vs])
            P.op("sp", lambda s, vs=vs, p=p: s.dma_start(
                out=vm_scr[:, p * 256:(p + 1) * 256].rearrange("(s p) n -> p s n", p=128), in_=v(vs)),
                reads=[vs], writes=[dr("vm_scr")], dma="st_" + vs.name)


    mark(2)
    xst = [sb("xst0", B0, [D]), sb("xst1", B0 + 16384, [D])]
    vst = sb("vst", B0, [4, 2048], BF16)
    mixedT = sb("mixedT", B0 + 16384, [16, T], BF16)
    qT = sb("qT", B0 + 32768, [HEADS, T], BF16)
    kst = sb("kst", B0 + 49152, [HEADS, T], BF16)
    cosb = sb("cosb", B0 + 65536, [T])
    sinb = sb("sinb", B0 + 67584, [T])
    invc = sb("invc", B0 + 69632, [4, T])
    qf = sb("qf", B0 + 77824, [T])
    qb = sb("qb", B0 + 79872, [T], BF16)
    t1 = sb("t1", B0 + 80896, [T])
    ropeset = [(qf, qb, t1), (sb("qf2", E0, [T]), sb("qb2", E0 + 2048, [T], BF16), sb("t12", E0 + 3072, [T]))]
    rrr = [0]
    hp = [sb("hp%d" % i, B0 + 82944 + i * 2112, [528]) for i in range(2)]

    def rope_consume(pb, dst_ap, dst_buf, ksum_h=None, blk0=None):
        qf, qb, t1 = ropeset[rrr[0] % 2]
        sw = PB[rrr[0] % 2]
        rrr[0] += 1
        P.op("act", lambda e: e.copy(out=v(qf), in_=pb.ap), reads=[pb], writes=[qf])
        P.op("dve", lambda e: e.tensor_copy(out=v(qb), in_=v(qf)), reads=[qf], writes=[qb])
        P.op("pe", lambda t: t.matmul(sw.ap, lhsT=v(pswap), rhs=v(qb), start=True, stop=True),
             reads=[pswap, qb], writes=[sw])
        P.op("dve", lambda e: e.tensor_tensor(out=v(t1), in0=sw.ap, in1=v(sinb), op=ALU.mult),
             reads=[sw, sinb], writes=[t1])
        P.op("dve", lambda e: e.tensor_tensor(out=v(qf), in0=v(qf), in1=v(cosb), op=ALU.mult),
             reads=[qf, cosb], writes=[qf])
        P.op("dve", lambda e: e.tensor_tensor(out=v(qf), in0=v(qf), in1=v(t1), op=ALU.add),
             reads=[qf, t1], writes=[qf])
        P.op("act", lambda e: e.copy(out=dst_ap, in_=v(qf)), reads=[qf], writes=[dst_buf])
        if ksum_h is not None:
            P.op("dve", lambda e: e.tensor_reduce(out=v(kmsum)[:, ksum_h, blk0:blk0 + 2],
                                                  in_=v(qf).rearrange("p (a b) -> p a b", a=2),
                                                  axis=AX.X, op=ALU.add),
                 reads=[qf], writes=[kmsum])

    def pool_chunk(ch, pb, save_only):
        h = hp[ch % 2]
        g = ch // 4
        w = WINS[g]
        P.op("act", lambda e: e.copy(out=v(h)[:, 16:528], in_=pb.ap), reads=[pb], writes=[h])
        P.op("pool", lambda e: e.tensor_copy(out=v(h)[:, 0:16], in_=v(halo)[:, ch, :]), reads=[halo], writes=[h])
        P.op("pool", lambda e: e.tensor_copy(out=v(halo)[:, ch, :], in_=v(h)[:, 512:528]), reads=[h], writes=[halo])
        if save_only:
            return
        sA, sB = pooltmp[ch % 2]
        P.op("dve", lambda e: e.tensor_tensor(out=v(sA)[:, 1:528], in0=v(h)[:, 1:528], in1=v(h)[:, 0:527], op=ALU.add),
             reads=[h], writes=[sA])
        cur, oth, width, lo = sA, sB, 2, 1
        while width < w:
            lo2 = lo + width
            P.op("dve", lambda e, cur=cur, oth=oth, width=width, lo2=lo2: e.tensor_tensor(
                out=v(oth)[:, lo2:528], in0=v(cur)[:, lo2:528], in1=v(cur)[:, lo2 - width:528 - width], op=ALU.add),
                reads=[cur], writes=[oth])
            cur, oth = oth, cur
            width *= 2
            lo = lo2
        assert lo <= 16
        P.op("pool", lambda e, cur=cur: e.tensor_tensor(out=v(cur)[:, 16:528], in0=v(cur)[:, 16:528],
                                                       in1=v(invc)[:, g, :], op=ALU.mult),
             reads=[cur, invc], writes=[cur])
        P.op("dve", lambda e, cur=cur: e.tensor_tensor(out=v(mixedT)[:, ch, :], in0=v(cur)[:, 16:528],
                                                      in1=v(h)[:, 16:528], op=ALU.subtract),
             reads=[cur, h], writes=[mixedT])

    pooltmp = [(sb("ptA%d" % i, W0 + 16384 + i * WS, [528]), sb("ptB%d" % i, W0 + 16384 + 2112 + i * WS, [528]))
               for i in range(2)]

    def do_tile(ti):
        own = ti >= nctx
        oi = ti - nctx
        s0 = ti * T
        xT = actA
        for sub in range(4):
            xs = xst[sub % 2]
            P.op("sp", lambda s, xs=xs, sub=sub: s.dma_start(out=v(xs), in_=xall[s0 + sub * 128:s0 + (sub + 1) * 128, :]),
                 writes=[xs], dma=xs.name)
            transpose_rows(xs, v(xs), xT, sub * 128)
        P.op("sp", lambda s: s.dma_start(out=v(cosb), in_=cos_d[:, s0:s0 + T]), writes=[cosb], dma="cos")
        P.op("sp", lambda s: s.dma_start(out=v(sinb), in_=sin_d[:, s0:s0 + T]), writes=[sinb], dma="sin")
        mark(31)
        if (own or ti == nctx - 1) and os.environ.get("KNOPOOL", "0") == "0":
            if own:
                P.op("sp", lambda s: s.dma_start(out=v(invc), in_=invc_d[:, :, oi * T:(oi + 1) * T]),
                     writes=[invc], dma="invc")
            proj_fm("in", xT, KC, 0, 16, lambda n, pb: pool_chunk(n, pb, not own))
        mark(32)
        if own:
            proj_fm("in", xT, KC, 2048, HEADS, lambda n, pb: rope_consume(pb, v(qT)[:, n, :], qT))
        proj_fm("in", xT, KC, int(os.environ.get("KC0", "4096")), HEADS,
                lambda n, pb: rope_consume(pb, v(kst)[:, n, :], kst, ksum_h=n, blk0=2 * ti))
        mark(33)
        P.op("sp", lambda s: s.dma_start(out=kt_scr[:, :, s0:s0 + T].rearrange("h d t -> d h t"), in_=v(kst)),
             reads=[kst], writes=[dr("kt_%d" % ti)], dma="kst")
        mark(34)
        loaded = {}

        def ensv(i):
            if i < 8 and i not in loaded:
                loaded[i] = load_panel("in", 0, KC, 6144 + i * 256, 256)
        ensv(0)
        for p in range(8):
            ensv(p + 1)
            slot = loaded.pop(p)
            pbs = [next_ps(), next_ps()]

            def f(t, slot=slot, pbs=pbs):
                ins = None
                for sub in range(4):
                    out = pbs[sub // 2].ap[:, (sub % 2) * 256:(sub % 2) * 256 + 256]
                    for kc in range(KC):
                        ins = t.matmul(out, lhsT=xT.ap[:, kc, sub * 128:(sub + 1) * 128], rhs=slot.ap[:, kc, 0:256],
                                       start=(kc == 0), stop=(kc == KC - 1))
                return ins
            P.op("pe", f, reads=[slot, xT], writes=pbs)
            for q in range(2):
                copy_op(ev_eng(), v(vst)[:, 2 * q:2 * q + 2, p * 256:(p + 1) * 256],
                        pbs[q].ap.rearrange("p (a b) -> p a b", a=2), [pbs[q]], [vst])
        P.op("sp", lambda s: s.dma_start(out=v_scr[s0:s0 + T, :].rearrange("(s p) n -> p s n", p=128), in_=v(vst)),
             reads=[vst], writes=[dr("v_%d" % ti)], dma="vst")
        P.op("dve", lambda e: e.tensor_scalar(out=v(kmT)[:, :, 2 * ti:2 * ti + 2], in0=v(kmsum)[:, :, 2 * ti:2 * ti + 2],
                                              scalar1=1.0 / 256.0, scalar2=None, op0=ALU.mult),
             reads=[kmsum], writes=[kmT])
        if not own:
            mark(3)
            return
        mark(4)

        mixT = actA
        for g in range(4):
            slot = wslot_s[wrr[0] % 2]
            wrr[0] += 1
            P.op("sp", lambda s, slot=slot, g=g: s.dma_start(
                out=slot.ap[:, 0:8, :].rearrange("p (a b) n -> p a (b n)", a=4),
                in_=wb["pool"][g * 512:(g + 1) * 512, :].rearrange("(k p) n -> p k n", p=128)),
                reads=[dr("wb_pool")], writes=[slot], dma="w_" + slot.name)
            wv = slot.ap[:, 0:8, :].rearrange("p (a b) n -> p a (b n)", a=4)
            for nn in range(4):
                pb = next_ps()

                def f(t, wv=wv, nn=nn, pb=pb, g=g):
                    ins = None
                    for kc in range(4):
                        ins = t.matmul(pb.ap, lhsT=wv[:, kc, nn * 128:(nn + 1) * 128], rhs=v(mixedT)[:, 4 * g + kc, :],
                                       start=(kc == 0), stop=(kc == 3))
                    return ins
                P.op("pe", f, reads=[slot, mixedT], writes=[pb])
                n = 4 * g + nn
                P.op("dve", lambda e, pb=pb, n=n: e.tensor_scalar(out=v(mixT)[:, n, :], in0=pb.ap,
                                                                scalar1=v(pscale)[:, n:n + 1], scalar2=None, op0=ALU.mult),
                     reads=[pb, pscale], writes=[mixT])

        mark(5)
        nkc = (ti + 1) * 4
        kvslots = [(sb("kts0", W0, [NSLOT], BF16), sb("vs0", W0 + 16384, [NSLOT // 128, 128], BF16)),
                   (sb("kts1", B0, [NSLOT], BF16), sb("vs1", B0 + 16384, [NSLOT // 128, 128], BF16))]
        pT = [sb("pT%d" % i, B0 + 49152 + i * 1024, [T], BF16) for i in range(3)]
        maskT = [sb("maskT%d" % i, B0 + 52224 + i * 1024, [T], BF16, parts=32) for i in range(2)]
        rinv = sb("rinv", B0 + 54272, [T])
        scale = 128.0 ** -0.5
        mb8 = [[sb("mb8_%d_%d" % (i, j), E0 + 5120 + (i * 4 + j) * 64, [NBLK], BF16) for j in range(4)] for i in range(2)]

        def load_kv(h):
            kts, vs = kvslots[h % 2]
            P.op("sp", lambda s: s.dma_start(out=v(kts)[:, 0:nkc * 128], in_=kt_scr[h, :, 0:nkc * 128]),
                 reads=[dr("kt_%d" % i) for i in range(ti + 1)], writes=[kts], dma="kv_" + kts.name)
            P.op("sp", lambda s: s.dma_start(
                out=v(vs)[:, 0:nkc, :],
                in_=v_scr[0:nkc * 128, h * 128:(h + 1) * 128].rearrange("(c p) d -> p c d", p=128)),
                reads=[dr("v_%d" % i) for i in range(ti + 1)], writes=[vs], dma="kv_" + vs.name)

        def mask_a(h):
            pg = PB[0]
            for sub in range(4):
                ob = 2 * oi + sub // 2
                mb = mb8[h % 2][sub]
                P.op("pe", lambda t, sub=sub: t.matmul(
                    pg.ap[:, sub * NBLK:(sub + 1) * NBLK], lhsT=v(qT)[:, h, sub * 128:(sub + 1) * 128], rhs=v(kmT)[:, h, :],
                    start=True, stop=True), reads=[qT, kmT], writes=[pg])
                P.op("dve", lambda e, sub=sub, ob=ob: e.tensor_tensor(out=v(g2), in0=pg.ap[:, sub * NBLK:(sub + 1) * NBLK],
                                                                     in1=v(vbg)[:, ob, :], op=ALU.add),
                     reads=[pg, vbg], writes=[g2])
                P.op("dve", lambda e: e.max(out=v(top8), in_=v(g2)), reads=[g2], writes=[top8])
                P.op("dve", lambda e: e.tensor_scalar(out=v(selb), in0=v(g2), scalar1=v(top8)[:, 2:3], scalar2=BIG,
                                                      op0=ALU.is_ge, op1=ALU.mult),
                     reads=[g2, top8], writes=[selb])
                P.op("dve", lambda e, ob=ob: e.tensor_tensor(out=v(selb), in0=v(selb), in1=v(vbgm)[:, ob, :], op=ALU.add),
                     reads=[selb, vbgm], writes=[selb])
                P.op("dve", lambda e, ob=ob, mb=mb: e.tensor_tensor(out=v(mb), in0=v(selb), in1=v(ndg)[:, ob, :],
                                                                   op=ALU.mult),
                     reads=[selb, ndg], writes=[mb])

        def mask_b(h):
            pg = PB[1]
            mT = maskT[h % 2]

            def f(t):
                ins = None
                for sub in range(4):
                    ins = t.transpose(out=pg.ap.bitcast(BF16)[0:NBLK, sub * 128:(sub + 1) * 128], in_=v(mb8[h % 2][sub]),
                                      identity=v(identb))
                return ins
            P.op("pe", f, reads=mb8[h % 2] + [identb], writes=[pg])
            P.op("act", lambda e: e.copy(out=v(mT), in_=pg.ap.bitcast(BF16)[0:NBLK, 0:T]), reads=[pg], writes=[mT])

        load_kv(0)
        mask_a(0)
        mask_b(0)
        if HEADS > 1:
            load_kv(1)
        pos_a = 1
        pos_b = max(2, (3 * nkc) // 4)
        for h in range(HEADS):
            kts, vs = kvslots[h % 2]
            mT = maskT[h % 2]
            psO, psR = (PB[5], PB[6]) if h % 2 == 0 else (PB[7], PB[2])
            sbanks = (3, 4)
            for step in range(nkc + 2):
                ci = step
                if ci < nkc:
                    if h + 1 < HEADS and ci == pos_a:
                        mask_a(h + 1)
                    if h + 1 < HEADS and ci == pos_b:
                        mask_b(h + 1)
                    j = ci // 2
                    pS = PB[sbanks[ci % 2]]
                    pt = pT[ci % 3]

                    def fs(t, pS=pS, ci=ci, j=j, kts=kts, mT=mT, h=h):
                        t.matmul(pS.ap, lhsT=v(kts)[:, ci * 128:(ci + 1) * 128], rhs=v(qT)[:, h, :], start=True, stop=False)
                        return t.matmul(pS.ap, lhsT=v(eblk)[:, j, :], rhs=v(mT), start=False, stop=True)
                    P.op("pe", fs, reads=[kts, qT, eblk, mT], writes=[pS])
                    P.op("act", lambda e, pS=pS, pt=pt: e.activation(out=v(pt), in_=pS.ap, func=AF.Exp, scale=scale),
                         reads=[pS], writes=[pt])
                    cl = ci - ti * 4
                    if cl >= 0:
                        P.op("pool", lambda e, pt=pt, cl=cl: e.tensor_tensor(out=v(pt), in0=v(pt), in1=v(cm)[:, cl, :],
                                                                            op=ALU.mult),
                             reads=[pt, cm], writes=[pt])
                if step >= 2:
                    cj = step - 2
                    ptj = pT[cj % 3]

                    def fo(t, pt=ptj, ci=cj, vs=vs, psO=psO, psR=psR):
                        t.matmul(psO.ap, lhsT=v(vs)[:, ci, :], rhs=v(pt), start=(ci == 0), stop=(ci == nkc - 1))
                        return t.matmul(psR.ap, lhsT=v(ones), rhs=v(pt), start=(ci == 0), stop=(ci == nkc - 1))
                    P.op("pe", fo, reads=[vs, ptj, ones], writes=[psO, psR])
            if h + 2 < HEADS:
                load_kv(h + 2)
            P.op("dve", lambda e, psR=psR: e.reciprocal(out=v(rinv), in_=psR.ap), reads=[psR], writes=[rinv])
            P.op("dve", lambda e, psO=psO, h=h: e.tensor_tensor(out=v(mixT)[:, 16 + h, :], in0=psO.ap, in1=v(rinv),
                                                               op=ALU.mult),
                 reads=[psO, rinv], writes=[mixT])

        mark(6)
        ystB = [sb("ystB%d_" % i, B0 + i * 4096, [4, 256]) for i in range(2)]
        ystA = [sb("ystA%d_" % i, A0 + i * 4096, [4, 256]) for i in range(2)]
        proj_tm("out", mixT, KC, ystB, dst=y_scr, dst_res=dr("y_scr"))
        lnb = ln_bufs()
        hT = actA
        ln_stage(dr("y_scr"), xall[s0:s0 + T, :], dr("xin"), 0, h_scr, dr("h_scr"), hT, lnb)
        mark(7)
        q2T = sb("q2T", B0, [KC, T], BF16)
        sc2 = 1024.0 ** -0.5
        proj_fm("xq", hT, KC, 0, KC, lambda n, pb: copy_op(ev_eng(), v(q2T)[:, n, :], pb.ap, [pb], [q2T]))
        kmTs = sb("kmTs", B0 + 32768, [KC, MEM], BF16)
        vms = sb("vms", B0 + 49152, [2, D], BF16)
        pT2 = [sb("pT2_%d" % i, B0 + 65536 + i * 2048, [2, T], BF16) for i in range(2)]
        rinv2 = sb("rinv2", B0 + 69632, [T])
        P.op("sp", lambda s: s.dma_start(out=v(kmTs), in_=kmT_scr.rearrange("(k p) m -> p k m", p=128)),
             reads=[dr("kmT_scr")], writes=[kmTs], dma="kmTs")
        P.op("sp", lambda s: s.dma_start(out=v(vms), in_=vm_scr.rearrange("(s p) n -> p s n", p=128)),
             reads=[dr("vm_scr")], writes=[vms], dma="vms")
        o2T = actA
        for xh in range(4):
            p2 = pT2[xh % 2]
            for mc in range(2):
                pS = next_ps((3, 4))

                def fs(t, pS=pS, mc=mc, xh=xh):
                    ins = None
                    for kc in range(8):
                        ins = t.matmul(pS.ap, lhsT=v(kmTs)[:, xh * 8 + kc, mc * 128:(mc + 1) * 128],
                                       rhs=v(q2T)[:, xh * 8 + kc, :], start=(kc == 0), stop=(kc == 7))
                    return ins
                P.op("pe", fs, reads=[kmTs, q2T], writes=[pS])
                P.op("act", lambda e, pS=pS, p2=p2, mc=mc: e.activation(out=v(p2)[:, mc, :], in_=pS.ap, func=AF.Exp,
                                                                       scale=sc2),
                     reads=[pS], writes=[p2])
            psR = PB[2]

            def fr(t, p2=p2, psR=psR):
                t.matmul(psR.ap, lhsT=v(ones), rhs=v(p2)[:, 0, :], start=True, stop=False)
                return t.matmul(psR.ap, lhsT=v(ones), rhs=v(p2)[:, 1, :], start=False, stop=True)
            P.op("pe", fr, reads=[ones, p2], writes=[psR])
            P.op("dve", lambda e, psR=psR: e.reciprocal(out=v(rinv2), in_=psR.ap), reads=[psR], writes=[rinv2])
            for dc in range(8):
                psO = next_ps((5, 6, 7))

                def fo(t, p2=p2, psO=psO, dc=dc, xh=xh):
                    c0 = xh * 1024 + dc * 128
                    t.matmul(psO.ap, lhsT=v(vms)[:, 0, c0:c0 + 128], rhs=v(p2)[:, 0, :], start=True, stop=False)
                    return t.matmul(psO.ap, lhsT=v(vms)[:, 1, c0:c0 + 128], rhs=v(p2)[:, 1, :], start=False, stop=True)
                P.op("pe", fo, reads=[vms, p2], writes=[psO])
                P.op("dve", lambda e, psO=psO, dc=dc, xh=xh: e.tensor_tensor(out=v(o2T)[:, xh * 8 + dc, :], in0=psO.ap,
                                                                            in1=v(rinv2), op=ALU.mult),
                     reads=[psO, rinv2], writes=[o2T])
        mark(8)
        proj_tm("xo", o2T, KC, ystB, dst=y_scr, dst_res=dr("y_scr"))
        h2T = actA
        ln_stage(dr("y_scr"), h_scr, dr("h_scr"), 2, h_scr, dr("h_scr"), h2T, lnb)
        mark(9)
        actT = sb("actT", B0, [FC, T], BF16)
        sg = [sb("sg%d" % i, W0 + 32768 + i * 2048, [T]) for i in range(2)]

        def load_f(name, n, slot):
            src = wb[name][:, n * 128:(n + 1) * 128].rearrange("(k p) n -> p k n", p=128)
            P.op("sp", lambda s: s.dma_start(out=slot.ap, in_=src), reads=[dr("wb_" + name)], writes=[slot],
                 dma="w_" + slot.name)
            return slot
        fl = {}

        def ensf(n):
            if n < FC and n not in fl:
                fl[n] = (load_f("gate", n, fslot[(n % 2) * 2]), load_f("up", n, fslot[(n % 2) * 2 + 1]))
        ensf(0)
        for n in range(FC):
            ensf(n + 1)
            sl_g, sl_u = fl.pop(n)
            pg, pu = next_ps(), next_ps()

            def fg(t, sl=sl_g, pb=pg):
                ins = None
                for kc in range(KC):
                    ins = t.matmul(pb.ap, lhsT=sl.ap[:, kc, 0:128], rhs=v(h2T)[:, kc, :], start=(kc == 0), stop=(kc == KC - 1))
                return ins
            P.op("pe", fg, reads=[sl_g, h2T], writes=[pg])

            def fu(t, sl=sl_u, pb=pu):
                ins = None
                for kc in range(KC):
                    ins = t.matmul(pb.ap, lhsT=sl.ap[:, kc, 0:128], rhs=v(h2T)[:, kc, :], start=(kc == 0), stop=(kc == KC - 1))
                return ins
            P.op("pe", fu, reads=[sl_u, h2T], writes=[pu])
            s_ = sg[n % 2]
            P.op("act", lambda e, s_=s_, pg=pg: e.activation(out=v(s_), in_=pg.ap, func=AF.Silu), reads=[pg], writes=[s_])
            P.op("dve", lambda e, s_=s_, pu=pu, n=n: e.tensor_tensor(out=v(actT)[:, n, :], in0=pu.ap, in1=v(s_), op=ALU.mult),
                 reads=[pu, s_], writes=[actT])
        mark(10)
        proj_tm("down", actT, FC, ystA, dst=y_scr, dst_res=dr("y_scr"))
        lnb = ln_bufs()
        ln_stage(dr("y_scr"), h_scr, dr("h_scr"), 4, out_d[oi * T:(oi + 1) * T, :], dr("out"), None, lnb)

    for ti_ in range(NT):
        if ti_ == nctx:
            mem_stage()
        do_tile(ti_)

    P.frozen = False
    P.final_op = P.op("sp", None, reads=[dr("out")])

    with nc.Block() as block:
        P.emit(nc, block, stack)
    stack.close()
    return nc


def _host_tables(qt, nctx=12, nown=4):
    NSLOT = (nctx + nown) * T
    nctx_tok = nctx * T
    nvalid_tok = min(qt * nown * T, nctx_tok)
    pos = np.zeros(NSLOT, np.int64)
    pos[nctx_tok - nvalid_tok:nctx_tok] = np.arange(nvalid_tok) + (qt * nown * T - nvalid_tok)
    pos[nctx_tok:] = qt * nown * T + np.arange(nown * T)
    half = 64
    inv = (np.float32(10000.0) ** (-np.arange(half, dtype=np.float32) / np.float32(half))).astype(np.float32)
    ang = pos.astype(np.float32)[:, None] * inv[None, :]
    cos = np.cos(ang).astype(np.float32).T
    sin = np.sin(ang).astype(np.float32).T
    cosT = np.ascontiguousarray(np.concatenate([cos, cos], 0))
    sinT = np.ascontiguousarray(np.concatenate([sin, sin], 0))
    gpos = qt * nown * T + np.arange(nown * T)
    invc = np.stack([1.0 / np.minimum(gpos + 1, w) for w in WINS]).astype(np.float32)
    invc = np.ascontiguousarray(np.broadcast_to(invc[None], (128, 4, nown * T)))
    nvalid_blk = nvalid_tok // 256
    vbg = np.full((8, NBLK), -BIG, np.float32)
    ndg = np.ones((8, NBLK), np.float32)
    for ob in range(2 * nown):
        vbg[ob, 2 * nctx - nvalid_blk:2 * nctx] = 0.0
        vbg[ob, 2 * nctx:2 * nctx + ob] = 0.0
        ndg[ob, 2 * nctx + ob] = 0.0
    vbg = np.ascontiguousarray(np.broadcast_to(vbg[None], (128, 8, NBLK)))
    ndg = np.ascontiguousarray(np.broadcast_to(ndg[None], (128, 8, NBLK)))
    return cosT, sinT, invc, vbg, ndg


def _consts():
    k = np.arange(128)[:, None]
    cmk = np.zeros((128, 4, T), np.float32)
    for cl in range(4):
        cmk[:, cl, :] = ((cl * 128 + k) <= np.arange(T)[None, :]).astype(np.float32)
    eb = np.zeros((32, NBLK, 128), np.float32)
    for j in range(NBLK):
        eb[j, j, :] = 1.0
    psw = np.zeros((128, 128), np.float32)
    for m in range(64):
        psw[m + 64, m] = -1.0
        psw[m, m + 64] = 1.0
    return cmk, eb, psw, np.eye(128, dtype=np.float32)


_NC_CACHE = {}


def kernel(x, mem, w_mix_in, w_pool, pool_scale, w_mix_out, ln1_g, ln1_b, w_xq, w_xkv, w_xo,
           ln2_g, ln2_b, w_gate, w_up, w_down, ln3_g, ln3_b):
    x = np.asarray(x, np.float32)
    mem = np.asarray(mem, np.float32)
    B, S, _ = x.shape
    nq = 4
    own = S // nq
    if "nc" not in _NC_CACHE:
        _NC_CACHE["nc"] = build()
    nc = _NC_CACHE["nc"]
    cmk, eb, psw, idn = _consts()
    lnp = np.ascontiguousarray(np.stack([np.asarray(a, np.float32)[0] for a in (ln1_g, ln1_b, ln2_g, ln2_b, ln3_g, ln3_b)]))
    psc = np.ascontiguousarray(np.asarray(pool_scale, np.float32)[0].reshape(16, 128).T)
    shared = {
        "w_in": np.asarray(w_mix_in, np.float32)[0], "w_pool": np.asarray(w_pool, np.float32)[0].reshape(2048, 512),
        "w_out": np.asarray(w_mix_out, np.float32)[0], "w_xq": np.asarray(w_xq, np.float32)[0],
        "w_xkv": np.asarray(w_xkv, np.float32)[0], "w_xo": np.asarray(w_xo, np.float32)[0],
        "w_gate": np.asarray(w_gate, np.float32)[0], "w_up": np.asarray(w_up, np.float32)[0],
        "w_down": np.asarray(w_down, np.float32)[0], "lnp": lnp, "pscale": psc,
        "cm": cmk, "eblk": eb, "pswap": psw, "ident": idn,
    }
    in_maps = []
    for c in range(8):
        b, qt = c // nq, c % nq
        xall = np.zeros((3 * own + own, D), np.float32)
        xall[3 * own - qt * own:3 * own] = x[b, :qt * own]
        xall[3 * own:] = x[b, qt * own:(qt + 1) * own]
        cosT, sinT, invc, vbg, ndg = _host_tables(qt)
        m = dict(shared)
        m.update({"xall": xall, "mem": mem[b], "cosT": cosT, "sinT": sinT, "invc": invc, "vbg": vbg, "ndg": ndg})
        in_maps.append(m)
    res = run_bass_kernel_spmd(nc, in_maps, core_ids=list(range(8)))
    out = np.empty((B, S, D), np.float32)
    for c in range(8):
        b, qt = c // nq, c % nq
        out[b, qt * own:(qt + 1) * own] = np.asarray(res.results[c]["out"], np.float32)
    return out
```
